# Optimizing a Trainium2 kernel written in Bass

```python
import math
import jax
import jax.numpy as jnp
from jax import lax
import numpy as np

D_MODEL = 1024
BATCH = 2
SEQ = 8192
DEPTH = 2

HEAD_DIM = 64
GROUP_WIDTH = D_MODEL // 4
MIX_WIDTH = 4 * GROUP_WIDTH
ML_HEADS = GROUP_WIDTH // HEAD_DIM
ML_CHUNK = 64
POOL_WINDOWS = (2, 4, 8, 16)
POOL_GROUPS = len(POOL_WINDOWS)
POOL_CH = GROUP_WIDTH // POOL_GROUPS
DIFF_HEADS = GROUP_WIDTH // HEAD_DIM
DIFF_QK_DIM = HEAD_DIM // 2
GQA_Q_HEADS = GROUP_WIDTH // HEAD_DIM
GQA_KV_HEADS = GQA_Q_HEADS // 2
GQA_GROUP = GQA_Q_HEADS // GQA_KV_HEADS
GRID_W = 64
ROPE_THETA = 10000.0
Q_BLOCK = 128
MEM_LEN = 256
CROSS_HEADS = 4
CROSS_HEAD_DIM = D_MODEL // CROSS_HEADS
D_FF = 256 * math.ceil(8 * D_MODEL / 3 / 256)
CONV_W = 3
EPS = 1e-6
IN_SIZES = (GROUP_WIDTH, GROUP_WIDTH, GROUP_WIDTH, GROUP_WIDTH, 4 * ML_HEADS,
            GROUP_WIDTH,
            DIFF_HEADS * 2 * DIFF_QK_DIM, DIFF_HEADS * 2 * DIFF_QK_DIM, DIFF_HEADS * HEAD_DIM,
            GQA_Q_HEADS * HEAD_DIM, GQA_KV_HEADS * HEAD_DIM, GQA_KV_HEADS * HEAD_DIM)
D_IN = sum(IN_SIZES)

kernel_name = 'hybrid_parallel_head_encoder'


def rms_norm(x, g):
    x32 = x.astype(jnp.float32)
    y = x32 * lax.rsqrt(jnp.mean(x32 * x32, axis=-1, keepdims=True) + EPS)
    return (y * g.astype(jnp.float32)).astype(x.dtype)


def rope_cos_sin(pos, dim):
    inv = ROPE_THETA ** (-jnp.arange(0, dim, 2, dtype=jnp.float32) / dim)
    ang = pos.astype(jnp.float32)[:, None] * inv[None, :]
    return jnp.cos(ang), jnp.sin(ang)


def apply_rope(x, cos, sin):
    d2 = x.shape[-1] // 2
    bshape = (cos.shape[0],) + (1,) * (x.ndim - 3) + (cos.shape[1],)
    c = cos.reshape(bshape).astype(x.dtype)
    s = sin.reshape(bshape).astype(x.dtype)
    x1, x2 = x[..., :d2], x[..., d2:]
    return jnp.concatenate([x1 * c - x2 * s, x2 * c + x1 * s], axis=-1)


def sweep_query_blocks(fn, q):
    B, S = q.shape[:2]
    nb = S // Q_BLOCK
    qb = jnp.moveaxis(q.reshape((B, nb, Q_BLOCK) + q.shape[2:]), 1, 0)
    out = lax.map(fn, qb)
    return jnp.moveaxis(out, 0, 1).reshape((B, S) + out.shape[3:])


def mlstm_chunkwise(q, k, v, li, lf):
    B, H, S, d = q.shape
    L = ML_CHUNK
    nc = S // L
    q = q.reshape(B, H, nc, L, d)
    k = k.reshape(B, H, nc, L, d)
    v = v.reshape(B, H, nc, L, d)
    li = li.reshape(B, H, nc, L)
    b = jnp.cumsum(lf.reshape(B, H, nc, L), axis=-1)
    g = b[..., -1]
    a = g[..., None] - b + li
    m_loc = jnp.max(a, axis=-1)
    w = jnp.exp(a - m_loc[..., None])
    C_loc = jnp.einsum('bhcl,bhcld,bhcle->bhcde', w, k, v)
    n_loc = jnp.einsum('bhcl,bhcld->bhcd', w, k)

    def step(carry, inp):
        C, n, m = carry
        g_c, m_l, C_l, n_l = inp
        m_new = jnp.maximum(g_c + m, m_l)
        f_old = jnp.exp(g_c + m - m_new)
        f_loc = jnp.exp(m_l - m_new)
        C_new = f_old[..., None, None] * C + f_loc[..., None, None] * C_l
        n_new = f_old[..., None] * n + f_loc[..., None] * n_l
        return (C_new, n_new, m_new), (C, n, m)

    init = (jnp.zeros((B, H, d, d), jnp.float32), jnp.zeros((B, H, d), jnp.float32),
            jnp.zeros((B, H), jnp.float32))
    xs = (jnp.moveaxis(g, 2, 0), jnp.moveaxis(m_loc, 2, 0),
          jnp.moveaxis(C_loc, 2, 0), jnp.moveaxis(n_loc, 2, 0))
    _, (C_prev, n_prev, m_prev) = lax.scan(step, init, xs)
    C_prev = jnp.moveaxis(C_prev, 0, 2)
    n_prev = jnp.moveaxis(n_prev, 0, 2)
    m_prev = jnp.moveaxis(m_prev, 0, 2)

    lower = jnp.tril(jnp.ones((L, L), dtype=bool))
    dlog = jnp.where(lower, b[..., :, None] - b[..., None, :] + li[..., None, :], -jnp.inf)
    e = b + m_prev[..., None]
    m_t = jnp.maximum(e, jnp.max(dlog, axis=-1))
    wts = jnp.exp(dlog - m_t[..., None]) * jnp.einsum('bhctd,bhcsd->bhcts', q, k)
    s_inter = jnp.exp(e - m_t)
    num = (jnp.einsum('bhcts,bhcsd->bhctd', wts, v)
           + s_inter[..., None] * jnp.einsum('bhctd,bhcde->bhcte', q, C_prev))
    den = jnp.sum(wts, axis=-1) + s_inter * jnp.einsum('bhctd,bhcd->bhct', q, n_prev)
    h = num / jnp.maximum(jnp.abs(den), jnp.exp(-m_t))[..., None]
    return h.reshape(B, H, S, d)


def mlstm_mixer(q, k, v, o, gates, b_i, b_f, g_norm):
    B, S, _ = q.shape
    f32 = jnp.float32
    heads = lambda z: z.astype(f32).reshape(B, S, ML_HEADS, HEAD_DIM).transpose(0, 2, 1, 3)
    qh, kh, vh = heads(q), heads(k) * (HEAD_DIM ** -0.5), heads(v)
    gt = gates.astype(f32).reshape(B, S, 4, ML_HEADS)
    bi = b_i.astype(f32)
    bf = b_f.astype(f32)
    li_f = (gt[:, :, 0] + bi[0]).transpose(0, 2, 1)
    lf_f = jax.nn.log_sigmoid(gt[:, :, 1] + bf[0]).transpose(0, 2, 1)
    li_b = (gt[:, :, 2] + bi[1]).transpose(0, 2, 1)
    lf_b = jax.nn.log_sigmoid(gt[:, :, 3] + bf[1]).transpose(0, 2, 1)
    h_f = mlstm_chunkwise(qh, kh, vh, li_f, lf_f)
    flip = lambda z: jnp.flip(z, axis=2)
    h_b = flip(mlstm_chunkwise(flip(qh), flip(kh), flip(vh), flip(li_b), flip(lf_b)))
    h = (h_f + h_b).transpose(0, 2, 1, 3)
    h = rms_norm(h, g_norm.reshape(ML_HEADS, HEAD_DIM))
    y = h.reshape(B, S, GROUP_WIDTH) * jax.nn.sigmoid(o.astype(f32))
    return y.astype(o.dtype)


def pool_mixer(u, w, scale):
    B, S, _ = u.shape
    ug = u.astype(jnp.float32).reshape(B, S, POOL_GROUPS, POOL_CH)
    cs = jnp.concatenate([jnp.zeros((B, 1, POOL_GROUPS, POOL_CH), jnp.float32),
                          jnp.cumsum(ug, axis=1)], axis=1)
    t = jnp.arange(S)
    outs = []
    for gi, win in enumerate(POOL_WINDOWS):
        lo = jnp.clip(t - win // 2, 0, S - 1)
        hi = jnp.clip(t - win // 2 + win - 1, 0, S - 1)
        cs_g = cs[:, :, gi]
        total = cs_g[:, hi + 1] - cs_g[:, lo]
        cnt = (hi - lo + 1).astype(jnp.float32)[None, :, None]
        outs.append(total / cnt - ug[:, :, gi])
    pooled = jnp.stack(outs, axis=2)
    y = jnp.einsum('bsgc,gce->bsge', pooled, w.astype(jnp.float32))
    return (y.reshape(B, S, GROUP_WIDTH) * scale.astype(jnp.float32)).astype(u.dtype)


def diff_attention(q, k, v, g_q, g_k, lam_params, g_sub, cos, sin, lam_init):
    B, S, _ = q.shape
    q = apply_rope(rms_norm(q.reshape(B, S, DIFF_HEADS, 2, DIFF_QK_DIM), g_q), cos, sin)
    k = apply_rope(rms_norm(k.reshape(B, S, DIFF_HEADS, 2, DIFF_QK_DIM), g_k), cos, sin)
    v = v.reshape(B, S, DIFF_HEADS, HEAD_DIM)
    lp = lam_params.astype(jnp.float32)
    lam = jnp.exp(jnp.sum(lp[0] * lp[1])) - jnp.exp(jnp.sum(lp[2] * lp[3])) + lam_init
    scale = DIFF_QK_DIM ** -0.5

    def block(qb):
        s = jnp.einsum('bqhcd,bshcd->bhcqs', qb, k).astype(jnp.float32) * scale
        p = jax.nn.softmax(s, axis=-1)
        pd = p[:, :, 0] - lam * p[:, :, 1]
        return jnp.einsum('bhqs,bshd->bqhd', pd.astype(v.dtype), v)

    o = sweep_query_blocks(block, q)
    o = rms_norm(o, g_sub) * (1.0 - lam_init)
    return o.reshape(B, S, GROUP_WIDTH)


def gqa_axial(q, k, v, g_q, g_k, row_cs, col_cs):
    B, S, _ = q.shape
    half = HEAD_DIM // 2
    def axial(z):
        return jnp.concatenate([apply_rope(z[..., :half], *row_cs),
                                apply_rope(z[..., half:], *col_cs)], axis=-1)
    q = axial(rms_norm(q.reshape(B, S, GQA_Q_HEADS, HEAD_DIM), g_q))
    k = axial(rms_norm(k.reshape(B, S, GQA_KV_HEADS, HEAD_DIM), g_k))
    v = v.reshape(B, S, GQA_KV_HEADS, HEAD_DIM)
    q = q.reshape(B, S, GQA_KV_HEADS, GQA_GROUP, HEAD_DIM)
    scale = HEAD_DIM ** -0.5

    def block(qb):
        s = jnp.einsum('bqkgd,bskd->bkgqs', qb, k).astype(jnp.float32) * scale
        p = jax.nn.softmax(s, axis=-1)
        return jnp.einsum('bkgqs,bskd->bqkgd', p.astype(v.dtype), v)

    o = sweep_query_blocks(block, q)
    return o.reshape(B, S, GROUP_WIDTH)


def cross_attention(xn, memn, w_q, w_kv, w_o, g_q, g_k):
    B, S, _ = xn.shape
    M = memn.shape[1]
    q = rms_norm((xn @ w_q).reshape(B, S, CROSS_HEADS, CROSS_HEAD_DIM), g_q)
    kv = (memn @ w_kv).reshape(B, M, 2, CROSS_HEADS, CROSS_HEAD_DIM)
    k = rms_norm(kv[:, :, 0], g_k)
    v = kv[:, :, 1]
    s = jnp.einsum('bshd,bmhd->bhsm', q, k).astype(jnp.float32) * (CROSS_HEAD_DIM ** -0.5)
    p = jax.nn.softmax(s, axis=-1)
    o = jnp.einsum('bhsm,bmhd->bshd', p.astype(v.dtype), v).reshape(B, S, D_MODEL)
    return o @ w_o


def conv_ffn(xn, w_in, conv_w, conv_b, w_out):
    h = xn @ w_in
    gate, up = h[..., :D_FF], h[..., D_FF:]
    gp = jnp.pad(gate, ((0, 0), (1, 1), (0, 0)))
    gconv = gp[:, :-2] * conv_w[0] + gp[:, 1:-1] * conv_w[1] + gp[:, 2:] * conv_w[2] + conv_b
    return (jax.nn.silu(gconv) * up) @ w_out


def setup_inputs(seed: int = 0) -> dict:
    key = jax.random.key(seed)
    keys = iter(jax.random.split(key, 40))
    f32 = jnp.float32
    def nrm(shape, scale):
        return scale * jax.random.normal(next(keys), shape, f32)
    def gain(shape):
        return 1.0 + 0.1 * jax.random.normal(next(keys), shape, f32)
    D = D_MODEL
    return {
        'x': jax.random.normal(next(keys), (BATCH, SEQ, D), f32),
        'mem': jax.random.normal(next(keys), (BATCH, MEM_LEN, D), f32),
        'norm_mix': gain((DEPTH, D)),
        'w_in': nrm((DEPTH, D, D_IN), D ** -0.5),
        'ml_bias_i': nrm((DEPTH, 2, ML_HEADS), 0.1),
        'ml_bias_f': 3.0 + 3.0 * jax.random.uniform(next(keys), (DEPTH, 2, ML_HEADS), f32),
        'ml_norm': gain((DEPTH, GROUP_WIDTH)),
        'pool_w': nrm((DEPTH, POOL_GROUPS, POOL_CH, POOL_CH), POOL_CH ** -0.5),
        'pool_scale': gain((DEPTH, GROUP_WIDTH)),
        'diff_qnorm': gain((DEPTH, DIFF_QK_DIM)),
        'diff_knorm': gain((DEPTH, DIFF_QK_DIM)),
        'diff_lambda': nrm((DEPTH, 4, DIFF_QK_DIM), 0.1),
        'diff_subnorm': gain((DEPTH, HEAD_DIM)),
        'gqa_qnorm': gain((DEPTH, HEAD_DIM)),
        'gqa_knorm': gain((DEPTH, HEAD_DIM)),
        'w_out': nrm((DEPTH, MIX_WIDTH, D), MIX_WIDTH ** -0.5),
        'norm_cross': gain((DEPTH, D)),
        'norm_mem': gain((DEPTH, D)),
        'w_cq': nrm((DEPTH, D, D), D ** -0.5),
        'w_ckv': nrm((DEPTH, D, 2 * D), D ** -0.5),
        'cross_qnorm': gain((DEPTH, CROSS_HEAD_DIM)),
        'cross_knorm': gain((DEPTH, CROSS_HEAD_DIM)),
        'w_co': nrm((DEPTH, D, D), D ** -0.5),
        'norm_ffn': gain((DEPTH, D)),
        'w_ffn_in': nrm((DEPTH, D, 2 * D_FF), D ** -0.5),
        'ffn_conv': nrm((DEPTH, CONV_W, D_FF), CONV_W ** -0.5),
        'ffn_conv_b': nrm((DEPTH, D_FF), 0.02),
        'w_ffn_out': nrm((DEPTH, D_FF, D), D_FF ** -0.5),
    }


def reference(x, mem, norm_mix, w_in, ml_bias_i, ml_bias_f, ml_norm, pool_w, pool_scale,
              diff_qnorm, diff_knorm, diff_lambda, diff_subnorm, gqa_qnorm, gqa_knorm, w_out,
              norm_cross, norm_mem, w_cq, w_ckv, cross_qnorm, cross_knorm, w_co,
              norm_ffn, w_ffn_in, ffn_conv, ffn_conv_b, w_ffn_out):
    B, S, _ = x.shape
    rows = S // GRID_W
    pos = jnp.arange(S)
    row = jnp.repeat(jnp.arange(rows), GRID_W)
    col = pos - row * GRID_W
    cos1, sin1 = rope_cos_sin(pos, DIFF_QK_DIM)
    row_cs = rope_cos_sin(row, HEAD_DIM // 2)
    col_cs = rope_cos_sin(col, HEAD_DIM // 2)
    splits, acc = [], 0
    for sz in IN_SIZES[:-1]:
        acc += sz
        splits.append(acc)

    for l in range(DEPTH):
        lam_init = 0.8 - 0.6 * math.exp(-0.3 * l)
        h = rms_norm(x, norm_mix[l])
        (ml_q, ml_k, ml_v, ml_o, ml_g, pool_u,
         d_q, d_k, d_v, g_q, g_k, g_v) = jnp.split(h @ w_in[l], splits, axis=-1)
        y_a = mlstm_mixer(ml_q, ml_k, ml_v, ml_o, ml_g, ml_bias_i[l], ml_bias_f[l], ml_norm[l])
        y_b = pool_mixer(pool_u, pool_w[l], pool_scale[l])
        y_c = diff_attention(d_q, d_k, d_v, diff_qnorm[l], diff_knorm[l], diff_lambda[l],
                             diff_subnorm[l], cos1, sin1, lam_init)
        y_d = gqa_axial(g_q, g_k, g_v, gqa_qnorm[l], gqa_knorm[l], row_cs, col_cs)
        x = x + jnp.concatenate([y_a, y_b, y_c, y_d], axis=-1) @ w_out[l]
        x = x + cross_attention(rms_norm(x, norm_cross[l]), rms_norm(mem, norm_mem[l]),
                                w_cq[l], w_ckv[l], w_co[l], cross_qnorm[l], cross_knorm[l])
        x = x + conv_ffn(rms_norm(x, norm_ffn[l]), w_ffn_in[l], ffn_conv[l], ffn_conv_b[l],
                         w_ffn_out[l])
    return x
```

```python
import math
from contextlib import ExitStack
import numpy as np
import concourse.bass as bass
import concourse.mybir as mybir
from concourse.bass_utils import run_bass_kernel_spmd

F32 = mybir.dt.float32
BF16 = mybir.dt.bfloat16
AF = mybir.ActivationFunctionType
ALU = mybir.AluOpType
AX = mybir.AxisListType

NCORES = 8
T = 2048
NT = 16
D = 1024
KC = 8
DIN = 2576
DFF = 2816
NFC = 22
EPS = 1e-6
L = 2
RG = [[0, 1, 2, 3], [4, 5, 6, 7]]
LN8 = math.log(0.125)
SAME_ENGINE_SYNC = True
VW = 66
SW = 2 * VW
MS = 2 * SW

PP_NMIX, PP_NCROSS, PP_NFFN, PP_NMEM = 0, 8, 16, 24
PP_DQ, PP_DK, PP_GQ, PP_GK, PP_DSUB = 32, 33, 34, 35, 36
PP_PSC = 37
PP_CQN = 39
PP_CKN = 41
PP_CONV = 43
PP_N = 43 + 88
BC_GB, BC_MLN, BC_LAM = 0, 16, 272
BC_N = 272 + 128
CM_MASKF, CM_MASKB, CM_CH0LO, CM_CH0HI, CM_CH1LO, CM_CH1HI, CM_CGE, CM_CLE, CM_ONES = range(9)
CM_BAND = 9
CM_BFIRST = 21
CM_BLAST = 25
CM_N = 29
CB_ID, CB_RPT, CB_B32, CB_B64, CB_ONES, CB_MASKF, CB_MASKB, CB_SEL = range(8)
CB_N = 8


class Ctr:
    __slots__ = ("sem", "val", "nb")

    def __init__(self, sem):
        self.sem = sem
        self.val = 0
        self.nb = False


class Buf:
    __slots__ = ("w", "r", "name", "excl")

    def __init__(self, name="", excl=False):
        self.w = None
        self.r = []
        self.name = name
        self.excl = excl


def bufs(n, name=""):
    return [Buf(f"{name}{i}") for i in range(n)]


class KB:
    def __init__(self, nc, es):
        self.nc = nc
        self.es = es
        self.eng = {"pe": nc.tensor, "act": nc.scalar, "dve": nc.vector, "pool": nc.gpsimd, "sp": nc.sync}
        self.allctr = []
        self.ectr = {e: self.newctr("c_" + e) for e in self.eng}
        self.seen = {e: {} for e in self.eng}
        self.nins = 0

    def newctr(self, name):
        c = Ctr(self.es.enter_context(self.nc.semaphore(name)))
        self.allctr.append(c)
        return c

    def _wait(self, e, need):
        eng = self.eng[e]
        for c, v in need.items():
            if c is self.ectr.get(e) and (e == "pe" or not SAME_ENGINE_SYNC):
                continue
            if self.seen[e].get(c, 0) < v:
                eng.wait_ge(c.sem, v)
                self.seen[e][c] = v

    def op(self, e, fn, rd=(), wr=(), dma=None, inc=None):
        need = {}
        for b in rd:
            if b.w is not None:
                need[b.w[0]] = max(need.get(b.w[0], 0), b.w[1])
            if b.excl:
                for t in b.r:
                    if t[0] is not self.ectr.get(e):
                        need[t[0]] = max(need.get(t[0], 0), t[1])
        for b in wr:
            if b.w is not None:
                need[b.w[0]] = max(need.get(b.w[0], 0), b.w[1])
            for t in b.r:
                need[t[0]] = max(need.get(t[0], 0), t[1])
        self._wait(e, need)
        ins = fn(self.eng[e])
        self.nins += 1
        if dma is not None:
            c = dma
            step = 16 if inc is None else inc
        else:
            c = self.ectr[e]
            step = 1
        c.val += step
        ins.then_inc(c.sem, step)
        t = (c, c.val)
        for b in rd:
            b.r = [x for x in b.r if x[0] is not c] + [t]
        for b in wr:
            b.w = t
            b.r = []
        return ins

    def barrier(self, engines=("pe", "act", "dve", "pool", "sp")):
        need = {c: c.val for c in self.allctr if c.val > 0 and not c.nb}
        for e in engines:
            self._wait(e, need)

    def mm(self, out, lhsT, rhs, rd, wr, start=True, stop=True, skip=False):
        return self.op("pe", lambda g: g.matmul(out, lhsT, rhs, start=start, stop=stop, skip_group_check=skip), rd, wr)

    def tr(self, out, in_, ident, rd, wr):
        return self.op("pe", lambda g: g.transpose(out, in_, ident), rd, wr)

    def act(self, out, in_, func, rd, wr, bias=0.0, scale=1.0, accum_out=None, e="act"):
        if accum_out is not None:
            return self.op(e, lambda g: g.activation(out, in_, func, bias=bias, scale=scale, accum_out=accum_out), rd, wr)
        return self.op(e, lambda g: g.activation(out, in_, func, bias=bias, scale=scale), rd, wr)

    def tt(self, out, in0, in1, op, rd, wr, e="dve"):
        return self.op(e, lambda g: g.tensor_tensor(out, in0, in1, op), rd, wr)

    def ts(self, out, in0, s1, op0, rd, wr, s2=None, op1=None, e="dve"):
        if op1 is None:
            return self.op(e, lambda g: g.tensor_scalar(out, in0, s1, None, op0), rd, wr)
        return self.op(e, lambda g: g.tensor_scalar(out, in0, s1, s2, op0, op1), rd, wr)

    def stt(self, out, in0, scalar, in1, op0, op1, rd, wr):
        return self.op("dve", lambda g: g.scalar_tensor_tensor(out, in0, scalar, in1, op0, op1), rd, wr)

    def copy(self, out, in_, rd, wr, e="dve"):
        if e == "act":
            return self.op("act", lambda g: g.copy(out, in_), rd, wr)
        return self.op(e, lambda g: g.tensor_copy(out, in_), rd, wr)

    def dma(self, e, out, in_, rd, wr, ctr):
        return self.op(e, lambda g: g.dma_start(out=out, in_=in_), rd, wr, dma=ctr)


def bcast(ap, shape):
    return ap.to_broadcast(list(shape))


def build(stop="full", dbg_cols=0, nl=L, rg=None):
    global RG
    RG = rg if rg is not None else [[0, 1, 2, 3], [4, 5, 6, 7]]
    nc = bass.Bass("TRN2", target_bir_lowering=False)
    dt = nc.dram_tensor
    declared = {}

    def din(name, shape):
        if name not in declared:
            declared[name] = dt(name, shape, F32, kind="ExternalInput").ap()
        return declared[name]

    nc._declared = declared
    x_d = din("x", [T, D])
    W = {
        "w_in": lambda: din("w_in", [nl, D, DIN]), "w_out": lambda: din("w_out", [nl, D, D]),
        "w_cq": lambda: din("w_cq", [nl, D, D]), "w_ckv": lambda: din("w_ckv", [nl, D, 2 * D]),
        "w_co": lambda: din("w_co", [nl, D, D]), "w_fi": lambda: din("w_ffn_in", [nl, D, 2 * DFF]),
        "w_fo": lambda: din("w_ffn_out", [nl, DFF, D]), "pool_w": lambda: din("pool_w", [nl, 4, 64, 64]),
        "mem": lambda: din("mem", [256, D]),
    }
    pp_d = din("pp", [nl, 128, PP_N])
    bc_d = din("bc", [nl, 128, BC_N])
    sel_d = din("sel", [128, 18])
    cmat_d = din("cmat", [CM_N, 128, 128])
    cmatb_d = din("cmatb", [CB_N, 128, 128])
    rope_d = din("rope", [4, 128, T])
    y_d = dt("y", [T, D] if stop == "full" else [128, 8], F32, kind="ExternalOutput").ap()
    dbg_d = dt("dbg", [128, dbg_cols], F32, kind="ExternalOutput").ap() if dbg_cols else None
    exK = [dt(f"exK{c}", [128, T], BF16, kind="Internal").ap() for c in range(3)]
    gaK = [dt(f"gaK{c}", [4 * 128, T], BF16, kind="Internal").ap() for c in range(3)]
    exV = [dt(f"exV{c}", [T, 128], BF16, kind="Internal").ap() for c in range(3)]
    gaV = [dt(f"gaV{c}", [4 * T, 128], BF16, kind="Internal").ap() for c in range(3)]
    exM = dt("exM", [128, 528], F32, kind="Internal").ap()
    gaM = dt("gaM", [4 * 128, 528], F32, kind="Internal").ap()
    exH = dt("exH", [128, 16], F32, kind="Internal").ap()
    gaH = dt("gaH", [4 * 128, 16], F32, kind="Internal").ap()

    with ExitStack() as es:
        k = KB(nc, es)

        uid = [0]

        def sb(name, shape, dtype, st=es):
            uid[0] += 1
            return st.enter_context(nc.sbuf_tensor(f"s_{name}_{uid[0]}", shape, dtype))

        x = sb("x", [128, NT, D], F32)
        xb = bufs(NT, "x")
        cm = sb("cm", [128, 9, 128], F32)
        cmb_ = Buf("cm")
        cb = sb("cb", [128, CB_N, 128], BF16)
        cbb = Buf("cb")
        sel = sb("sel", [128, 18], F32)
        selb = Buf("sel")
        pp = sb("pp", [128, PP_N], F32)
        ppb = Buf("pp")
        bc = sb("bc", [128, BC_N], F32)
        bcb = Buf("bc")
        ps = [es.enter_context(nc.psum_tensor(f"ps{i}", [128, 512], F32)) for i in range(8)]
        psb = [Buf(f"ps{i}", excl=True) for i in range(8)]
        ld = k.newctr("ld0")

        def psbf(i):
            return ps[i][:].bitcast(BF16)

        xv = x_d.rearrange("(i p) d -> p i d", p=128)
        xc = [k.newctr(f"xld{q}") for q in range(4)]
        for q in range(4):
            k.dma("sp", x[:, 4 * q:4 * q + 4, :], xv[:, 4 * q:4 * q + 4, :], [], xb[4 * q:4 * q + 4], xc[q])
        k.dma("sp", cm[:], cmat_d[0:9].rearrange("m p c -> p m c"), [], [cmb_], ld)
        cbc = k.newctr("cbld")
        k.dma("pool", cb[:], cmatb_d.rearrange("m p c -> p m c"), [], [cbb], cbc)
        selc = k.newctr("selld")
        k.dma("sp", sel[:], sel_d, [], [selb], selc)
        ident = cb[:, CB_ID, :]

        dbg_state = {"col": 0}
        dbgc = k.newctr("dbg")

        def dump(ap2d, rd, ncols, parts=128):
            c0 = dbg_state["col"]
            k.dma("sp", dbg_d[0:parts, c0:c0 + ncols], ap2d, rd, [], dbgc)
            dbg_state["col"] = c0 + ncols

        def finish():
            k.barrier()

        norm_tmp = (sb("nss", [128, NT], F32), Buf(), sb("nln", [128, NT], F32), Buf(), sb("nrs", [128, NT], F32), Buf(),
                    [sb(f"njunk{j}", [128, D], BF16) for j in range(2)], bufs(2))

        def norm_T(ph, hT, hTb, gcol, tag):
            ss, ssb, lnv, lnb, rstd, rsb, junk, jb = norm_tmp
            for i in range(NT):
                k.act(junk[i % 2][:], x[:, i, :], AF.Square, [xb[i]], [jb[i % 2], ssb], accum_out=ss[:, i:i + 1])
            k.act(lnv[:], ss[:], AF.Ln, [ssb], [lnb], scale=1.0 / D, bias=EPS)
            k.act(rstd[:], lnv[:], AF.Exp, [lnb], [rsb], scale=-0.5)
            for i in range(NT):
                hn = junk[i % 2]
                k.ts(hn[:], x[:, i, :], rstd[:, i:i + 1], ALU.mult, [xb[i], rsb], [jb[i % 2]])
                pb = 6 + (i % 2)
                pv = psbf(pb)
                for kc in range(KC):
                    k.tr(pv[:, kc * 128:(kc + 1) * 128], hn[:, kc * 128:(kc + 1) * 128], ident, [jb[i % 2], cbb], [psb[pb]])
                k.tt(hT[:, :, i * 128:(i + 1) * 128], pv.rearrange("p (k t) -> p k t", k=KC),
                     bcast(pp[:, gcol:gcol + KC].unsqueeze(2), [128, KC, 128]), ALU.mult,
                     [psb[pb], ppb], [hTb[i]])

        ctrs = {}

        def C(name):
            if name not in ctrs:
                ctrs[name] = k.newctr(name)
            return ctrs[name]

        def wv_(name, l, c0, c1):
            return W[name]()[l].rearrange("(k p) c -> p k c", p=128)[:, :, c0:c1]

        def dump_bf(ph, ap2d, rdb, n):
            tmp = sb("dtmp", [128, 512], F32, ph)
            tb_ = Buf()
            for q in range(n // 512):
                k.copy(tmp[:], ap2d[:, q * 512:(q + 1) * 512], rdb, [tb_])
                dump(tmp[:], [tb_], 512)

        exKb, gaKb, exVb, gaVb = bufs(3), bufs(3), bufs(3), bufs(3)
        exMb, gaMb, exHb, gaHb = Buf(), Buf(), Buf(), Buf()

        def allgather(ex, ga, exb, gab, name):
            C(name).nb = True
            k.op("pool", lambda g: g.collective_compute("AllGather", ALU.bypass, replica_groups=RG, ins=[ex], outs=[ga]),
                 [exb], [gab], dma=C(name), inc=1)

        def resid_add(i, half, pbank, rd_extra=()):
            k.tt(x[:, i, half * 512:(half + 1) * 512], x[:, i, half * 512:(half + 1) * 512], ps[pbank][:, :], ALU.add,
                 [psb[pbank], xb[i]] + list(rd_extra), [xb[i]])

        def load_wo(ph, wname, l, r0, nk, tag):
            wo = sb("wo" + tag, [128, nk, D], BF16, ph)
            wob = Buf()
            k.dma("pool", wo[:], W[wname]()[l][r0:r0 + nk * 128, :].rearrange("(k p) c -> p k c", p=128), [], [wob], C("wo" + tag))
            return wo, wob

        def out_proj(ph, yT, yTb, wname, l, r0, nk, tag, pre=None):
            wo, wob = pre if pre is not None else load_wo(ph, wname, l, r0, nk, tag)
            n = 0
            for i in range(NT):
                for half in range(2):
                    pb = 6 + (n % 2)
                    n += 1
                    for kc in range(nk):
                        k.mm(ps[pb][:, :], yT[:, kc, i * 128:(i + 1) * 128], wo[:, kc, half * 512:(half + 1) * 512],
                             [yTb, wob], [psb[pb]], start=(kc == 0), stop=(kc == nk - 1))
                    resid_add(i, half, pb)

        for l in range(nl):
            k.dma("sp", pp[:], pp_d[l], [], [ppb], C("pp"))
            k.dma("sp", bc[:], bc_d[l], [], [bcb], C("bc"))
            lam_init = 0.8 - 0.6 * math.exp(-0.3 * l)
            with ExitStack() as phQ:
              qTd = sb("qTd", [128, 2, T], BF16, phQ)
              qTdb = Buf()
              qTg = sb("qTg", [128, 2, T], BF16, phQ)
              qTgb = Buf()
              phU = ExitStack()
              phQ.push(phU)
              u = sb("u", [128, NT, 256], F32, phU)
              ub = bufs(NT, "u")
              with ExitStack() as phA:
                hT = sb("hT", [128, KC, T], BF16, phA)
                hTb = bufs(NT, "hT")
                gp = sb("gp", [128, NT, 32], F32, phA)
                gpb = Buf()
                alp = sb("alp", [128, NT, 8], F32, phA)
                alpb = Buf()
                bet = sb("bet", [128, NT, 8], F32, phA)
                betb = Buf()
                gam = sb("gam", [128, NT, 8], F32, phA)
                gamb = Buf()
                with ExitStack() as ph:
                    norm_T(ph, hT, hTb, PP_NMIX, "a")
                if stop == "norm1":
                    dump_bf(phA, hT[:, 0, :], hTb, T)
                    finish()
                    return nc
                with ExitStack() as ph1:
                    wqk = sb("wqk", [128, KC, 7 * 128], BF16, ph1)
                    wqkb = Buf()
                    wc = C("wqk")
                    k.dma("pool", wqk[:, :, 0:512], wv_("w_in", l, 1296, 1808), [], [wqkb], wc)
                    for (dst, src) in ((512, 2064), (576, 2192), (640, 2128), (704, 2256)):
                        k.dma("pool", wqk[:, :, dst:dst + 64], wv_("w_in", l, src, src + 64), [], [wqkb], wc)
                    k.dma("pool", wqk[:, :, 768:896], wv_("w_in", l, 2320, 2448), [], [wqkb], wc)
                    rt = sb("rt", [128, 4, 512], F32, ph1)
                    rtb = Buf()
                    kst = sb("kst", [128, 3, T], BF16, ph1)
                    kstb = Buf()
                    sq = [sb("sq", [128, 512], BF16, ph1) for _ in range(2)]
                    lnv = [sb("lnq", [128, 512], F32, ph1) for _ in range(2)]
                    rs = [sb("rsq", [128, 512], F32, ph1) for _ in range(2)]
                    kn = [sb("kn", [128, 512], BF16, ph1) for _ in range(2)]
                    t1 = [sb("t1", [128, 512], F32, ph1) for _ in range(2)]
                    t2 = [sb("t2", [128, 512], F32, ph1) for _ in range(2)]
                    sqb, lnb, rsb, knb, t1b, t2b = bufs(2), bufs(2), bufs(2), bufs(2), bufs(2), bufs(2)
                    specs = [(qTd, qTdb, 0, PP_DQ, CB_B32, 32, 0), (qTd, qTdb, 1, PP_DQ, CB_B32, 32, 0),
                             (kst, kstb, 0, PP_DK, CB_B32, 32, 0), (kst, kstb, 1, PP_DK, CB_B32, 32, 0),
                             (qTg, qTgb, 0, PP_GQ, CB_B64, 64, 2), (qTg, qTgb, 1, PP_GQ, CB_B64, 64, 2),
                             (kst, kstb, 2, PP_GK, CB_B64, 64, 2)]
                    n = 0
                    for tb in range(4):
                        tsl = slice(tb * 512, (tb + 1) * 512)
                        k.dma("sp", rt[:], rope_d[:, :, tsl].rearrange("m p t -> p m t"), [], [rtb], C("rt"))
                        for ci, (dst, dstb, dc, gcol, blk, dd, tab) in enumerate(specs):
                            a = n % 2
                            n += 1
                            praw, pss, prot = ps[a], ps[2 + a], ps[4 + a]
                            for kc in range(KC):
                                k.mm(praw[:, :], wqk[:, kc, ci * 128:(ci + 1) * 128], hT[:, kc, tsl],
                                     [wqkb] + hTb[4 * tb:4 * tb + 4], [psb[a]], start=(kc == 0), stop=(kc == KC - 1))
                            lim = int(stop[2:]) if (stop.startswith("qk") and len(stop) > 2) else 99
                            def bail():
                                finish()
                                return nc
                            if lim <= 0:
                                return bail()
                            k.act(sq[a][:], praw[:, :], AF.Square, [psb[a]], [sqb[a]])
                            if lim <= 1:
                                return bail()
                            k.mm(pss[:, :], cb[:, blk, :], sq[a][:], [cbb, sqb[a]], [psb[2 + a]])
                            k.act(lnv[a][:], pss[:, :], AF.Ln, [psb[2 + a]], [lnb[a]], scale=1.0 / dd, bias=EPS)
                            k.act(rs[a][:], lnv[a][:], AF.Exp, [lnb[a]], [rsb[a]], scale=-0.5)
                            if lim <= 2:
                                return bail()
                            k.stt(kn[a][:], praw[:, :], pp[:, gcol:gcol + 1], rs[a][:], ALU.mult, ALU.mult,
                                  [psb[a], ppb, rsb[a]], [knb[a]])
                            if lim <= 3:
                                return bail()
                            k.mm(prot[:, :], cb[:, CB_RPT, :], kn[a][:], [cbb, knb[a]], [psb[4 + a]])
                            if lim <= 4:
                                return bail()
                            k.tt(t1[a][:], kn[a][:], rt[:, tab, :], ALU.mult, [knb[a], rtb], [t1b[a]], e="pool")
                            if lim <= 5:
                                return bail()
                            k.tt(t2[a][:], prot[:, :], rt[:, tab + 1, :], ALU.mult, [psb[4 + a], rtb], [t2b[a]])
                            k.tt(dst[:, dc, tsl], t1[a][:], t2[a][:], ALU.add, [t1b[a], t2b[a]], [dstb], e="pool")
                            if lim <= 6:
                                return bail()
                    for c in range(3):
                        k.dma("sp", exK[c], kst[:, c, :], [kstb], [exKb[c]], C(f"exK{c}"))
                    if stop == "qk":
                        dump_bf(ph1, qTd[:, 0, :], [qTdb], T)
                        dump_bf(ph1, qTg[:, 1, :], [qTgb], T)
                        dump_bf(ph1, kst[:, 2, :], [kstb], T)
                        finish()
                        return nc
                    k.barrier()
                wml = sb("wml", [128, KC, 1040], BF16, phA)
                wmlb = Buf()
                with ExitStack() as ph1:
                    wv = sb("wv", [128, KC, 384], BF16, ph1)
                    wvb = Buf()
                    k.dma("pool", wv[:, :, 0:256], wv_("w_in", l, 1808, 2064), [], [wvb], C("wv"))
                    k.dma("pool", wv[:, :, 256:384], wv_("w_in", l, 2448, 2576), [], [wvb], C("wv"))
                    k.dma("pool", wml[:], wv_("w_in", l, 0, 1040), [], [wmlb], C("wml"))
                    wpl = sb("wpl", [128, KC, 256], BF16, ph1)
                    wplb = Buf()
                    k.dma("pool", wpl[:], wv_("w_in", l, 1040, 1296), [], [wplb], C("wpl"))
                    vst = sb("vst", [128, NT, 384], BF16, ph1)
                    vstb = Buf()
                    for i in range(NT):
                        a = i % 2
                        for kc in range(KC):
                            k.mm(ps[a][:, 0:384], hT[:, kc, i * 128:(i + 1) * 128], wv[:, kc, :], [hTb[i], wvb], [psb[a]],
                                 start=(kc == 0), stop=(kc == KC - 1))
                        k.copy(vst[:, i, :], ps[a][:, 0:384], [psb[a]], [vstb], e="act")
                    for c in range(3):
                        k.dma("sp", exV[c].rearrange("(i p) c -> p i c", p=128), vst[:, :, c * 128:(c + 1) * 128],
                              [vstb], [exVb[c]], C(f"exV{c}"))
                    for i in range(NT):
                        a = 2 + i % 2
                        for kc in range(KC):
                            k.mm(ps[a][:, 0:256], hT[:, kc, i * 128:(i + 1) * 128], wpl[:, kc, :], [hTb[i], wplb], [psb[a]],
                                 start=(kc == 0), stop=(kc == KC - 1))
                        k.copy(u[:, i, :], ps[a][:, 0:256], [psb[a]], [ub[i]], e="act")
                    if stop == "p1a":
                        finish()
                        return nc
                    zt = [sb("zt", [128, 16], F32, ph1) for _ in range(2)]
                    ztb = bufs(2)
                    et = [sb("et", [128, 2, 4], F32, ph1) for _ in range(2)]
                    etb = bufs(2)
                    spA = sb("spA", [128, NT, 8], F32, ph1)
                    spb = bufs(NT)
                    liA = sb("liA", [128, NT, 8], F32, ph1)
                    liAb = Buf()
                    for i in range(NT):
                        a = i % 2
                        pg = ps[4 + a]
                        for kc in range(KC):
                            k.mm(pg[:, 0:16], hT[:, kc, i * 128:(i + 1) * 128], wml[:, kc, 1024:1040], [hTb[i], wmlb], [psb[4 + a]],
                                 start=(kc == 0), stop=(kc == KC - 1))
                        k.tt(zt[a][:], pg[:, 0:16], bc[:, BC_GB:BC_GB + 16], ALU.add, [psb[4 + a], bcb], [ztb[a]])
                        z4 = zt[a][:].rearrange("p (d k h) -> p d k h", d=2, k=2)
                        k.copy(liA[:, i, :].rearrange("p (d h) -> p d h", d=2), z4[:, :, 0, :], [ztb[a]], [liAb])
                        k.act(et[a][:], z4[:, :, 1, :], AF.Exp, [ztb[a]], [etb[a]], scale=-1.0)
                        sp3 = spA[:, i, :].rearrange("p (d h) -> p d h", d=2)
                        k.act(sp3, et[a][:], AF.Ln, [etb[a]], [spb[i]], bias=1.0)
                        pq = ps[6 + a]
                        sp4 = spA[:, i, :].rearrange("p (d a b) -> p d a b", d=2, a=2)
                        rdg = [spb[i], cmb_]
                        k.mm(pq[:, 0:4], cm[:, CM_MASKF, :], sp3[:, 0, :], rdg, [psb[6 + a]])
                        k.mm(pq[:, 4:8], cm[:, CM_MASKB, :], sp3[:, 1, :], rdg, [psb[6 + a]])
                        for c in range(2):
                            o4 = pq[:, 8 + 4 * c:12 + 4 * c].rearrange("p (d a) -> p d a", d=2)
                            k.mm(o4, cm[:, CM_CH0LO + 2 * c, :], sp4[:, :, :, 0], rdg, [psb[6 + a]], start=True, stop=False)
                            k.mm(o4, cm[:, CM_CH0HI + 2 * c, :], sp4[:, :, :, 1], rdg, [psb[6 + a]], start=False, stop=True)
                        k.mm(pq[:, 16:20], cm[:, CM_CGE, :], sp3[:, 0, :], rdg, [psb[6 + a]])
                        k.mm(pq[:, 20:24], cm[:, CM_CLE, :], sp3[:, 1, :], rdg, [psb[6 + a]])
                        k.mm(pq[:, 24:32], cm[:, CM_ONES, :], spA[:, i, :], rdg, [psb[6 + a]])
                        k.copy(gp[:, i, :], pq[:, 0:32], [psb[6 + a]], [gpb])
                    if stop == "p1b":
                        finish()
                        return nc
                    tmpb = sb("tmpb", [128, NT, 8], F32, ph1)
                    tmpbb = Buf()
                    wex = sb("wex", [128, NT, 8], F32, ph1)
                    wexb = Buf()
                    suf = sb("suf", [128, NT, 8], F32, ph1)
                    sufb = Buf()
                    k.act(alp[:], gp[:, :, 0:8], AF.Exp, [gpb], [alpb], scale=-1.0)
                    k.tt(tmpb[:], liA[:], gp[:, :, 0:8], ALU.add, [liAb, gpb], [tmpbb])
                    k.act(bet[:], tmpb[:], AF.Exp, [tmpbb], [betb], bias=LN8)
                    k.act(gam[:], gp[:, :, 8:16], AF.Exp, [gpb], [gamb], scale=-1.0)
                    if stop == "p1c1":
                        finish()
                        return nc
                    k.op("pool", lambda g: g.memset(suf[:], 0.0), [], [sufb])
                    for i in range(NT - 2, -1, -1):
                        k.tt(suf[:, i, 0:4], suf[:, i + 1, 0:4], gp[:, i + 1, 24:28], ALU.add, [sufb, gpb], [sufb], e="pool")
                    for i in range(1, NT):
                        k.tt(suf[:, i, 4:8], suf[:, i - 1, 4:8], gp[:, i - 1, 28:32], ALU.add, [sufb, gpb], [sufb], e="pool")
                    if stop == "p1c2":
                        finish()
                        return nc
                    k.tt(wex[:], tmpb[:], gp[:, :, 16:24], ALU.subtract, [tmpbb, gpb], [wexb])
                    k.tt(wex[:], wex[:], suf[:], ALU.subtract, [wexb, sufb], [wexb])
                    k.act(wex[:], wex[:], AF.Exp, [wexb], [wexb], bias=LN8)
                    if stop == "p1c3":
                        finish()
                        return nc
                    msum = sb("msum", [128, MS + 4], F32, ph1)
                    msumb = Buf()
                    tot = sb("tot", [128, 8], F32, ph1)
                    totb = Buf()
                    k.tt(tot[:, 0:4], suf[:, 0, 0:4], gp[:, 0, 24:28], ALU.add, [sufb, gpb], [totb])
                    k.tt(tot[:, 4:8], suf[:, NT - 1, 4:8], gp[:, NT - 1, 28:32], ALU.add, [sufb, gpb], [totb])
                    tot4 = tot[:].rearrange("p (d a b) -> p d a b", d=2, a=2)
                    m3 = msum[:, MS:MS + 4].rearrange("p (d a) -> p d a", d=2)
                    k.copy(m3[0:64], tot4[0:64, :, :, 0], [totb], [msumb])
                    k.copy(m3[64:128], tot4[64:128, :, :, 1], [totb], [msumb])
                    if stop == "p1c":
                        finish()
                        return nc
                    kw = [[sb("kw", [128, 4, 64], BF16, ph1) for _ in range(2)] for _ in range(2)]
                    kwb = [bufs(2), bufs(2)]
                    vx = [sb("vx", [128, 4, VW], BF16, ph1) for _ in range(2)]
                    vxb = bufs(2)
                    for a in range(2):
                        k.op("pool", lambda g, a=a: g.memset(vx[a][:], 1.0), [], [vxb[a]])
                    psum_b = 0
                    k.op("pool", lambda g: g.memset(msum[:, 0:MS], 0.0), [], [msumb])
                    for i in range(NT):
                        first = {0: True, 1: True}
                        a = 0 if stop == "s4z" else i % 2
                        pk = ps[2 + a]
                        for kc in range(KC):
                            k.mm(pk[:, :], hT[:, kc, i * 128:(i + 1) * 128], wml[:, kc, 256:768], [hTb[i], wmlb], [psb[2 + a]],
                                 start=(kc == 0), stop=(kc == KC - 1))
                        if stop == "s1" and i == 1:
                            finish()
                            return nc
                        k4 = pk[:, 0:256].rearrange("p (h d) -> p h d", h=4)
                        for dr in range(2):
                            k.tt(kw[dr][a][:], k4, bcast(wex[:, i, 4 * dr:4 * dr + 4].unsqueeze(2), [128, 4, 64]), ALU.mult,
                                 [psb[2 + a], wexb], [kwb[dr][a]])
                        if stop == "s2" and i == 1:
                            finish()
                            return nc
                        k.copy(vx[a][:, :, 0:64], pk[:, 256:512].rearrange("p (h d) -> p h d", h=4), [psb[2 + a], kwb[0][a], kwb[1][a]], [vxb[a]], e="act")
                        if stop == "s3" and i == 1:
                            finish()
                            return nc
                        for dr in range(2):
                            for h in range(4):
                                half = h % 2
                                st_ = first[half]
                                first[half] = False
                                c0 = dr * SW + (h // 2) * VW
                                k.mm(ps[psum_b][half * 64:(half + 1) * 64, c0:c0 + VW], kw[dr][a][:, h, :], vx[a][:, h, :],
                                     [kwb[dr][a], vxb[a]], [psb[psum_b]], start=st_, stop=(dr == 1 and h >= 2), skip=True)
                        k.tt(msum[:, 0:MS], msum[:, 0:MS], ps[psum_b][:, 0:MS], ALU.add, [msumb, psb[psum_b]], [msumb])
                        if stop in ("s4", "s4z") and i == 1:
                            finish()
                            return nc
                    if stop == "s5":
                        finish()
                        return nc
                    if stop == "s6":
                        finish()
                        return nc
                    if stop == "p1d1":
                        dump(msum[:], [msumb], MS + 4)
                        finish()
                        return nc
                    k.dma("sp", exM[:, 0:MS + 4], msum[:], [msumb], [exMb], C("exM"))
                    k.dma("sp", exM[0:8, MS + 4:MS + 260], u[0:8, 0, :], [ub[0]], [exMb], C("exM"))
                    k.dma("sp", exM[120:128, MS + 4:MS + 260], u[120:128, NT - 1, :], [ub[NT - 1]], [exMb], C("exM"))
                    if stop == "p1d":
                        finish()
                        return nc
                    allgather(exM, gaM, exMb, gaMb, "agM")
                    for c in range(3):
                        allgather(exK[c], gaK[c], exKb[c], gaKb[c], f"agK{c}")
                    for c in range(3):
                        allgather(exV[c], gaV[c], exVb[c], gaVb[c], f"agV{c}")
                    if stop == "p1":
                        dump(gp[:].rearrange("p i c -> p (i c)"), [gpb], 512)
                        dump(msum[:], [msumb], MS + 4)
                        dump(wex[:].rearrange("p i c -> p (i c)"), [wexb], 128)
                        finish()
                        return nc
                    k.barrier()
                if stop == "m0":
                    finish()
                    return nc
                with ExitStack() as ph2:
                    yT = sb("yT", [128, 2, T], BF16, ph2)
                    yTb = Buf()
                    C0 = sb("C0", [128, 2, SW], F32, ph2)
                    C0b = Buf()
                    phM = ExitStack()
                    ph2.push(phM)
                    M4 = sb("M4", [128, 4, MS + 4], F32, phM)
                    M4b = Buf()
                    k.dma("sp", M4[:], gaM.rearrange("(r p) c -> p r c", p=128)[:, :, 0:MS + 4], [gaMb], [M4b], C("M4"))
                    E4 = sb("E4", [128, 4, 4], F32, phM)
                    E4b = Buf()
                    k.act(E4[:], M4[:, :, MS:MS + 4], AF.Exp, [M4b], [E4b], scale=-1.0)
                    acc = sb("acc", [128, SW], F32, phM)
                    accb = Buf()
                    k.op("dve", lambda g: g.memset(C0[:], 0.0), [], [C0b])
                    for dr in range(2):
                        k.op("dve", lambda g: g.memset(acc[:], 0.0), [], [accb])
                        order = range(4) if dr == 0 else range(3, -1, -1)
                        for r in order:
                            k.stt(C0[:, dr, :], acc[:], sel[:, r:r + 1], C0[:, dr, :], ALU.mult, ALU.add, [accb, selb, C0b], [C0b])
                            a3 = acc[:].rearrange("p (a e) -> p a e", a=2)
                            k.tt(a3, a3, bcast(E4[:, r, 2 * dr:2 * dr + 2].unsqueeze(2), [128, 2, VW]), ALU.mult, [accb, E4b], [accb])
                            k.tt(acc[:], acc[:], M4[:, r, dr * SW:(dr + 1) * SW], ALU.add, [accb, M4b], [accb])
                    k.barrier()
                    phM.close()
                    if stop == "m1":
                        finish()
                        return nc
                    CbA = sb("CbA", [128, NT, 2, SW], BF16, ph2)
                    CbAb = Buf()
                    Sb = sb("Sb", [128, SW], F32, ph2)
                    Sbb = Buf()
                    St = sb("St", [128, SW], F32, ph2)
                    Stb = Buf()
                    betm = sb("betm", [128, NT, 2, 8], F32, ph2)
                    betmb = Buf()
                    for c in range(2):
                        k.ts(betm[:, :, c, :], bet[:], cm[:, CM_CH0LO + 2 * c, 0:1], ALU.mult, [betb, cmb_], [betmb])
                    ks = [[sb("ks", [128, 4, 64], BF16, ph2) for _ in range(2)] for _ in range(2)]
                    ksb = [bufs(2), bufs(2)]
                    vx = [sb("vx2", [128, 4, VW], BF16, ph2) for _ in range(2)]
                    vxb = bufs(2)
                    for a in range(2):
                        k.op("dve", lambda g, a=a: g.memset(vx[a][:], 1.0), [], [vxb[a]])
                    k.copy(Sb[:], C0[:, 1, :], [C0b], [Sbb])
                    for i in range(NT - 1, -1, -1):
                        a = i % 2
                        pk = ps[a]
                        for kc in range(KC):
                            k.mm(pk[:, :], hT[:, kc, i * 128:(i + 1) * 128], wml[:, kc, 256:768], [hTb[i], wmlb], [psb[a]],
                                 start=(kc == 0), stop=(kc == KC - 1))
                        for c in range(2):
                            k.tt(ks[a][c][:], pk[:, 0:256].rearrange("p (h d) -> p h d", h=4),
                                 bcast(betm[:, i, c, 4:8].unsqueeze(2), [128, 4, 64]), ALU.mult, [psb[a], betmb], [ksb[a][c]])
                        k.copy(vx[a][:, :, 0:64], pk[:, 256:512].rearrange("p (h d) -> p h d", h=4), [psb[a]], [vxb[a]], e="act")
                        pd = ps[2 + a]
                        for c in range(2):
                            for h in range(4):
                                half = h % 2
                                c0 = c * SW + (h // 2) * VW
                                k.mm(pd[half * 64:(half + 1) * 64, c0:c0 + VW], ks[a][c][:, h, :],
                                     vx[a][:, h, :], [ksb[a][c], vxb[a]], [psb[2 + a]])
                        if stop == "m2a":
                            finish()
                            return nc
                        for c in (1, 0):
                            if stop == "m2b" and c == 0:
                                finish()
                                return nc
                            k.copy(CbA[:, i, c, :], Sb[:], [Sbb], [CbAb], e="act")
                            k.tt(St[:], Sb[:], pd[:, c * SW:(c + 1) * SW], ALU.add, [Sbb, psb[2 + a]], [Stb])
                            gidx = c * 4 + 2
                            k.tt(Sb[:].rearrange("p (a e) -> p a e", a=2), St[:].rearrange("p (a e) -> p a e", a=2),
                                 bcast(gam[:, i, gidx:gidx + 2].unsqueeze(2), [128, 2, VW]), ALU.mult, [Stb, gamb], [Sbb])
                    if stop == "m2":
                        finish()
                        return nc
                    Sf = sb("Sf", [128, SW], F32, ph2)
                    Sfb = Buf()
                    Cz = [[[sb("Cz", [128, SW], BF16, ph2) for _ in range(2)] for _ in range(2)] for _ in range(2)]
                    Czb = [[bufs(2), bufs(2)], [bufs(2), bufs(2)]]
                    ksm = [[sb("ksm", [128, 4, 64], BF16, ph2) for _ in range(2)] for _ in range(2)]
                    ksmb = [bufs(2), bufs(2)]
                    k.copy(Sf[:], C0[:, 0, :], [C0b], [Sfb])
                    qs = [[sb("qs", [128, 4, 64], BF16, ph2) for _ in range(2)] for _ in range(2)]
                    qsb = [bufs(2), bufs(2)]
                    ks2 = [[sb("ks2", [128, 4, 64], BF16, ph2) for _ in range(2)] for _ in range(2)]
                    ks2b = [bufs(2), bufs(2)]
                    T8 = [sb("T8", [128, 8, 128], BF16, ph2)] * 2
                    T8b = [Buf()] * 2
                    Am = [[sb("Am", [128, 2, 2, 128], BF16, ph2) for _ in range(2)]] * 2
                    Amb = [bufs(2)] * 2
                    eo = [sb("eo", [128, 256], F32, ph2) for _ in range(2)]
                    eob = bufs(2)
                    den = sb("den", [128, 2, 4], F32, ph2)
                    denb = Buf()
                    hh = [sb("hh", [128, 4, 64], F32, ph2) for _ in range(2)]
                    hhb = bufs(2)
                    hsq = sb("hsq", [128, 4, 64], F32, ph2)
                    hsqb = Buf()
                    ssm = sb("ssm", [128, 4], F32, ph2)
                    ssmb = Buf()
                    yv = sb("yv", [128, 256], F32, ph2)
                    yvb = Buf()
                    yb16 = sb("yb16", [128, 256], BF16, ph2)
                    yb16b = Buf()
                    pqk, pvo = ps[0], ps[1]

                    def emit_proj(i):
                        for kc in range(KC):
                            k.mm(pqk[:, :], hT[:, kc, i * 128:(i + 1) * 128], wml[:, kc, 0:512], [hTb[i], wmlb], [psb[0]],
                                 start=(kc == 0), stop=(kc == KC - 1))
                        for kc in range(KC):
                            k.mm(pvo[:, :], hT[:, kc, i * 128:(i + 1) * 128], wml[:, kc, 512:1024], [hTb[i], wmlb], [psb[1]],
                                 start=(kc == 0), stop=(kc == KC - 1))

                    emit_proj(0)
                    for i in range(NT):
                        a = i % 2
                        q4 = pqk[:, 0:256].rearrange("p (h d) -> p h d", h=4)
                        k4 = pqk[:, 256:512].rearrange("p (h d) -> p h d", h=4)
                        for dr in range(2):
                            k.tt(qs[dr][a][:], q4, bcast(alp[:, i, 4 * dr:4 * dr + 4].unsqueeze(2), [128, 4, 64]), ALU.mult,
                                 [psb[0], alpb], [qsb[dr][a]])
                            k.tt(ks2[dr][a][:], k4, bcast(bet[:, i, 4 * dr:4 * dr + 4].unsqueeze(2), [128, 4, 64]), ALU.mult,
                                 [psb[0], betb], [ks2b[dr][a]])
                        for c in range(2):
                            k.tt(ksm[a][c][:], k4, bcast(betm[:, i, c, 0:4].unsqueeze(2), [128, 4, 64]), ALU.mult,
                                 [psb[0], betmb], [ksmb[a][c]])
                        k.copy(vx[a][:, :, 0:64], pvo[:, 0:256].rearrange("p (h d) -> p h d", h=4), [psb[1]], [vxb[a]], e="act")
                        k.act(eo[a][:], pvo[:, 256:512], AF.Exp, [psb[1]], [eob[a]], scale=-1.0)
                        pT = psbf(2)
                        srcs = [(qs[0][a], qsb[0][a]), (ks2[0][a], ks2b[0][a]), (qs[1][a], qsb[1][a]), (ks2[1][a], ks2b[1][a])]
                        for si, (sa, sab) in enumerate(srcs):
                            for pr in range(2):
                                j = si * 2 + pr
                                k.tr(pT[:, j * 128:(j + 1) * 128], sa[:, 2 * pr:2 * pr + 2, :].rearrange("p h d -> p (h d)"), ident,
                                     [sab, cbb], [psb[2]])
                        k.copy(T8[a][:].rearrange("p j t -> p (j t)"), pT[:, 0:1024], [psb[2]], [T8b[a]])
                        for half in range(2):
                            pA = ps[3 + half]
                            hs_ = slice(half * 64, half * 64 + 64)
                            for dr in range(2):
                                for pr in range(2):
                                    cc = (dr * 2 + pr) * 128
                                    k.mm(pA[:, cc:cc + 128], T8[a][hs_, 4 * dr + 2 + pr, :], T8[a][hs_, 4 * dr + pr, :],
                                         [T8b[a]], [psb[3 + half]])
                            k.tt(Am[a][half][:], pA[:, :].rearrange("p (d j t) -> p d j t", d=2, j=2),
                                 bcast(cm[:, CM_MASKF:CM_MASKF + 2, :].unsqueeze(2), [128, 2, 2, 128]), ALU.mult,
                                 [psb[3 + half], cmb_], [Amb[a][half]])
                        pd = ps[5]
                        for c in range(2):
                            for h in range(4):
                                half = h % 2
                                c0 = c * SW + (h // 2) * VW
                                k.mm(pd[half * 64:(half + 1) * 64, c0:c0 + VW], ksm[a][c][:, h, :], vx[a][:, h, :],
                                     [ksmb[a][c], vxb[a]], [psb[5]])
                        if i + 1 < NT:
                            emit_proj(i + 1)
                        for c in range(2):
                            for half in range(2):
                                k.ts(Cz[0][c][half][:], Sf[:], cm[:, CM_CH0LO + 2 * half, 0:1], ALU.mult, [Sfb, cmb_], [Czb[0][c][half]])
                                k.ts(Cz[1][c][half][:], CbA[:, i, c, :], cm[:, CM_CH0LO + 2 * half, 0:1], ALU.mult, [CbAb, cmb_], [Czb[1][c][half]])
                            k.tt(St[:], Sf[:], pd[:, c * SW:(c + 1) * SW], ALU.add, [Sfb, psb[5]], [Stb])
                            gidx = c * 4
                            k.tt(Sf[:].rearrange("p (a e) -> p a e", a=2), St[:].rearrange("p (a e) -> p a e", a=2),
                                 bcast(gam[:, i, gidx:gidx + 2].unsqueeze(2), [128, 2, VW]), ALU.mult, [Stb, gamb], [Sfb])
                        for dr in range(2):
                            po = ps[6 + dr]
                            for h in range(4):
                                half = h % 2
                                k.mm(po[:, h * VW:(h + 1) * VW], Am[a][half][:, dr, h // 2, :], vx[a][:, h, :], [Amb[a][half], vxb[a]], [psb[6 + dr]],
                                     start=True, stop=False)
                                for c in range(2):
                                    k.mm(po[c * 64:(c + 1) * 64, h * VW:(h + 1) * VW], T8[a][:, 4 * dr + h // 2, c * 64:(c + 1) * 64],
                                         Cz[dr][c][half][:, (h // 2) * VW:(h // 2) * VW + VW],
                                         [T8b[a], Czb[dr][c][half]], [psb[6 + dr]], start=False, stop=(c == 1), skip=True)
                            po3 = po[:, 0:4 * VW].rearrange("p (h e) -> p h e", h=4)
                            k.ts(den[:, dr, :], po3[:, :, 64], -1.0, ALU.mult, [psb[6 + dr]], [denb])
                            k.stt(den[:, dr, :], po3[:, :, 64], 1.0, den[:, dr, :], ALU.max, ALU.max, [psb[6 + dr], denb], [denb])
                            k.op("dve", lambda g, dr=dr: g.reciprocal(den[:, dr, :], den[:, dr, :]), [denb], [denb])
                            k.tt(hh[dr][:], po3[:, :, 0:64], bcast(den[:, dr, :].unsqueeze(2), [128, 4, 64]), ALU.mult,
                                 [psb[6 + dr], denb], [hhb[dr]])
                        k.tt(hh[0][:], hh[0][:], hh[1][:], ALU.add, [hhb[0], hhb[1]], [hhb[0]])
                        k.tt(hsq[:], hh[0][:], hh[0][:], ALU.mult, [hhb[0]], [hsqb])
                        k.op("dve", lambda g: g.tensor_reduce(ssm[:], hsq[:], AX.X, ALU.add), [hsqb], [ssmb])
                        k.act(ssm[:], ssm[:], AF.Ln, [ssmb], [ssmb], scale=1.0 / 64, bias=EPS)
                        k.act(ssm[:], ssm[:], AF.Exp, [ssmb], [ssmb], scale=-0.5)
                        k.tt(hh[0][:], hh[0][:], bcast(ssm[:].unsqueeze(2), [128, 4, 64]), ALU.mult, [hhb[0], ssmb], [hhb[0]])
                        k.tt(yv[:], hh[0][:].rearrange("p h d -> p (h d)"), bc[:, BC_MLN:BC_MLN + 256], ALU.mult, [hhb[0], bcb], [yvb])
                        k.ts(eo[a][:], eo[a][:], 1.0, ALU.add, [eob[a]], [eob[a]])
                        k.op("dve", lambda g, a=a: g.reciprocal(eo[a][:], eo[a][:]), [eob[a]], [eob[a]])
                        k.tt(yb16[:], yv[:], eo[a][:], ALU.mult, [yvb, eob[a]], [yb16b])
                        pT2 = psbf(2)
                        for pr in range(2):
                            k.tr(pT2[:, pr * 128:(pr + 1) * 128], yb16[:, pr * 128:(pr + 1) * 128], ident, [yb16b, cbb], [psb[2]])
                        k.copy(yT[:, :, i * 128:(i + 1) * 128], pT2[:, 0:256].rearrange("p (j t) -> p j t", j=2), [psb[2]], [yTb], e="act")
                    if stop == "ml":
                        dump_bf(ph2, yT[:, 0, :], [yTb], T)
                        dump_bf(ph2, yT[:, 1, :], [yTb], T)
                        finish()
                        return nc
                    out_proj(ph2, yT, yTb, "w_out", l, 0, 2, "a")
                    k.barrier()
              k.barrier()
              with ExitStack() as ph3:
                cm2 = sb("cm2", [128, 20, 128], F32, ph3)
                cm2b = Buf()
                k.dma("sp", cm2[:], cmat_d[9:29].rearrange("m p c -> p m c"), [], [cm2b], C("cm2"))
                pw = sb("pw", [64, 4, 64], F32, ph3)
                pwb = Buf()
                k.dma("sp", pw[:], W["pool_w"]()[l].rearrange("g c e -> c g e"), [], [pwb], C("pw"))
                HL = sb("HL", [128, 4, 256], F32, ph3)
                HR = sb("HR", [128, 4, 256], F32, ph3)
                HLb, HRb = Buf(), Buf()
                k.op("pool", lambda g: g.memset(HL[:], 0.0), [], [HLb])
                k.op("pool", lambda g: g.memset(HR[:], 0.0), [], [HRb])
                gM3 = gaM.rearrange("(r p) c -> p r c", p=128)
                k.dma("sp", HL[120:128, :, :], gM3[120:128, :, MS + 4:MS + 260], [gaMb], [HLb], C("HL"))
                k.dma("sp", HR[0:8, :, :], gM3[0:8, :, MS + 4:MS + 260], [gaMb], [HRb], C("HR"))
                uL = sb("uL", [128, 256], F32, ph3)
                uR = sb("uR", [128, 256], F32, ph3)
                uLb, uRb = Buf(), Buf()
                for (dst, dstb, src, srcb, s0) in ((uL, uLb, HL, HLb, 4), (uR, uRb, HR, HRb, 8)):
                    k.ts(dst[:], src[:, 0, :], sel[:, s0:s0 + 1], ALU.mult, [srcb, selb], [dstb])
                    for r in range(1, 4):
                        k.stt(dst[:], src[:, r, :], sel[:, s0 + r:s0 + r + 1], dst[:], ALU.mult, ALU.add, [srcb, selb, dstb], [dstb])
                yTp = sb("yTp", [128, 2, T], BF16, ph3)
                yTpb = Buf()
                pl = [sb("pl", [64, 4, 128], F32, ph3) for _ in range(2)]
                plb = bufs(2)
                for i in range(NT):
                    a = i % 2
                    pp_ = ps[a]
                    for g_ in range(4):
                        gs = slice(g_ * 64, (g_ + 1) * 64)
                        up = (uL[:, gs], uLb) if i == 0 else (u[:, i - 1, gs], ub[i - 1])
                        un = (uR[:, gs], uRb) if i == NT - 1 else (u[:, i + 1, gs], ub[i + 1])
                        if i == 0:
                            bcur = cm2[:, 12 + g_, :]
                        elif i == NT - 1:
                            bcur = cm2[:, 16 + g_, :]
                        else:
                            bcur = cm2[:, 3 * g_ + 1, :]
                        o_ = pp_[0:64, g_ * 128:(g_ + 1) * 128]
                        k.mm(o_, up[0], cm2[:, 3 * g_ + 0, :], [up[1], cm2b], [psb[a]], start=True, stop=False)
                        k.mm(o_, u[:, i, gs], bcur, [ub[i], cm2b], [psb[a]], start=False, stop=False)
                        k.mm(o_, un[0], cm2[:, 3 * g_ + 2, :], [un[1], cm2b], [psb[a]], start=False, stop=True)
                    k.copy(pl[a][:].rearrange("p g t -> p (g t)"), pp_[0:64, :], [psb[a]], [plb[a]], e="act")
                    py = ps[2 + a]
                    for g_ in range(4):
                        k.mm(py[(g_ % 2) * 64:(g_ % 2) * 64 + 64, (g_ // 2) * 128:(g_ // 2) * 128 + 128], pw[:, g_, :], pl[a][:, g_, :],
                             [pwb, plb[a]], [psb[2 + a]])
                    k.tt(yTp[:, :, i * 128:(i + 1) * 128], py[:, 0:256].rearrange("p (j t) -> p j t", j=2),
                         bcast(pp[:, PP_PSC:PP_PSC + 2].unsqueeze(2), [128, 2, 128]), ALU.mult, [psb[2 + a], ppb], [yTpb])
                if stop == "pool":
                    dump_bf(ph3, yTp[:, 0, :], [yTpb], T)
                    dump_bf(ph3, yTp[:, 1, :], [yTpb], T)
                    finish()
                    return nc
                out_proj(ph3, yTp, yTpb, "w_out", l, 256, 2, "p")
                k.barrier()
              phU.close()
              with ExitStack() as ph4:
                wo_pre = {"d": load_wo(ph4, "w_out", l, 512, 2, "td"), "g": load_wo(ph4, "w_out", l, 768, 2, "tg")}
                Kb = [sb("Kb", [128, 4 * T], BF16, ph4) for _ in range(2)]
                Kbb = bufs(2)
                Vb = [sb("Vb", [128, 64, VW], BF16, ph4) for _ in range(2)]
                Vbb = bufs(2)
                for a in range(2):
                    k.op("pool", lambda g, a=a: g.memset(Vb[a][:], 1.0), [], [Vbb[a]])
                pt = [sb("pt", [128, 512], BF16, ph4) for _ in range(6)]
                ptb = bufs(6)
                Ohl = [sb("Ohl", [VW, 2, 512], BF16, ph4) for _ in range(2)]
                Ohlb = bufs(2)
                for a in range(2):
                    k.op("dve", lambda g, a=a: g.memset(Ohl[a][:], 0.0), [], [Ohlb[a]])
                rec = sb("rec", [64, 512], F32, ph4)
                recb = Buf()
                on = [sb("on", [64, 512], F32, ph4) for _ in range(2)]
                onb = bufs(2)
                od = sb("od", [64, 512], F32, ph4)
                odb = Buf()
                osq = sb("osq", [64, 512], BF16, ph4)
                osqb = Buf()
                olr = sb("olr", [64, 512], F32, ph4)
                olrb = Buf()
                ybf = [sb("ybf", [64, 512], BF16, ph4) for _ in range(2)]
                ybfb = bufs(2)
                yTa = sb("yTa", [128, 2, T], BF16, ph4)
                yTab = Buf()
                lmt = sb("lmt", [128, 2, 32], F32, ph4)
                lmtb = Buf()
                lm2 = sb("lm2", [128, 4], F32, ph4)
                lm2b = Buf()
                l4 = bc[:, BC_LAM:BC_LAM + 128].rearrange("p (a b e) -> p a b e", a=2, b=2)
                k.tt(lmt[:], l4[:, :, 0, :], l4[:, :, 1, :], ALU.mult, [bcb], [lmtb])
                k.op("dve", lambda g: g.tensor_reduce(lm2[:, 0:2], lmt[:], AX.X, ALU.add), [lmtb], [lm2b])
                k.act(lm2[:, 0:2], lm2[:, 0:2], AF.Exp, [lm2b], [lm2b])
                k.tt(lm2[:, 2:3], lm2[:, 1:2], lm2[:, 0:1], ALU.subtract, [lm2b], [lm2b])
                k.ts(lm2[:, 2:3], lm2[:, 2:3], -lam_init, ALU.add, [lm2b], [lm2b])
                k.ts(lm2[:, 3:4], pp[:, PP_DSUB:PP_DSUB + 1], 1.0 - lam_init, ALU.mult, [ppb], [lm2b])
                qz = [sb("qz", [128, 512], BF16, ph4) for _ in range(2)]
                qzb = bufs(2)
                def dstream(h, cp):
                    return (h // 2, 12 + (h % 2) * 2 + cp, qTd, qTdb, h // 2, h // 2, h % 2, 32 ** -0.5)

                def gstream(hq):
                    kv = hq // 2
                    return (2, 16 + kv, qTg, qTgb, hq % 2, 2, kv, 64 ** -0.5)

                groups = [("d", [dstream(h, 0), dstream(h, 1)], [(h // 2, h % 2)]) for h in range(4)]
                groups += [("g", [gstream(0), gstream(2)], [(0, 0), (1, 0)]), ("g", [gstream(1), gstream(3)], [(0, 1), (1, 1)])]
                nk_ = 0
                nv_ = 0
                curK = None
                vcache = {}
                for phase_kind in ("d", "g"):
                    for (kind, streams, outs) in groups:
                        if kind != phase_kind:
                            continue
                        kc_ = streams[0][0]
                        if curK != kc_:
                            ka = nk_ % 2
                            nk_ += 1
                            k.dma("sp", Kb[ka][:].rearrange("p (r t) -> p r t", r=4), gaK[kc_].rearrange("(r p) t -> p r t", p=128),
                                  [gaKb[kc_]], [Kbb[ka]], C(f"Kb{ka}"))
                            curK = kc_
                            curKa = ka
                        vas = []
                        for st_ in streams:
                            key = (st_[5], st_[6])
                            if kind == "d" and vas:
                                vas.append(vas[0])
                                continue
                            va = nv_ % 2
                            nv_ += 1
                            k.dma("sp", Vb[va][:, :, 0:64], gaV[key[0]].rearrange("(kt p) e -> p kt e", p=128)[:, :, key[1] * 64:(key[1] + 1) * 64],
                                  [gaVb[key[0]]], [Vbb[va]], C(f"Vb{va}"))
                            vas.append(va)
                        sbanks = ((0, 1, 6), (2, 3, 7))
                        for qb in range(4):
                            qsl = slice(qb * 512, (qb + 1) * 512)
                            for si, st_ in enumerate(streams):
                                (kc2, mcol, qT_, qTb_, qc, vc, vh, scl) = st_
                                k.ts(qz[si][:], qT_[:, qc, qsl], sel[:, mcol:mcol + 1], ALU.mult, [qTb_, selb], [qzb[si]])
                            seq = [(kt, si) for kt in range(64) for si in range(2)]
                            LA = 4

                            def emitS(j):
                                kt, si = seq[j]
                                scl = streams[si][7]
                                sbank = sbanks[si][kt % 3]
                                pi = si * 3 + kt % 3
                                k.mm(ps[sbank][:, :], Kb[curKa][:, kt * 128:(kt + 1) * 128], qz[si][:],
                                     [Kbb[curKa], qzb[si]], [psb[sbank]])
                                k.act(pt[pi][:], ps[sbank][:, :], AF.Exp, [psb[sbank]], [ptb[pi]], scale=scl)

                            def emitPV(j):
                                kt, si = seq[j]
                                pi = si * 3 + kt % 3
                                k.mm(ps[4 + si][0:VW, :], Vb[vas[si]][:, kt, :], pt[pi][:], [Vbb[vas[si]], ptb[pi]], [psb[4 + si]],
                                     start=(kt == 0), stop=(kt == 63))

                            for j in range(LA):
                                emitS(j)
                            for j in range(len(seq)):
                                if j + LA < len(seq):
                                    emitS(j + LA)
                                emitPV(j)
                            for cp in range(2):
                                pO_ = ps[4 + cp]
                                k.copy(Ohl[cp][64:66, 0, :], pO_[64:66, :], [psb[4 + cp]], [Ohlb[cp]])
                                k.tt(Ohl[cp][64:66, 1, :], pO_[64:66, :], Ohl[cp][64:66, 0, :], ALU.subtract, [psb[4 + cp], Ohlb[cp]], [Ohlb[cp]])
                                for hl in range(2):
                                    k.mm(ps[0][0:64, :], cb[0:VW, CB_SEL, 0:64], Ohl[cp][:, hl, :], [cbb, Ohlb[cp]], [psb[0]],
                                         start=(hl == 0), stop=(hl == 1))
                                k.op("dve", lambda g: g.reciprocal(rec[:], ps[0][0:64, :]), [psb[0]], [recb])
                                k.tt(on[cp][:], pO_[0:64, :], rec[:], ALU.mult, [psb[4 + cp], recb], [onb[cp]])
                            ys = []
                            if kind == "d":
                                k.stt(od[:], on[1][:], lm2[0:64, 2:3], on[0][:], ALU.mult, ALU.add, [onb[0], onb[1], lm2b], [odb])
                                k.act(osq[:], od[:], AF.Square, [odb], [osqb])
                                k.mm(ps[1][0:64, :], cb[0:64, CB_ONES, 0:64], osq[:], [cbb, osqb], [psb[1]])
                                k.act(olr[:], ps[1][0:64, :], AF.Ln, [psb[1]], [olrb], scale=1.0 / 64, bias=EPS)
                                k.act(olr[:], olr[:], AF.Exp, [olrb], [olrb], scale=-0.5)
                                k.stt(ybf[0][:], od[:], lm2[0:64, 3:4], olr[:], ALU.mult, ALU.mult, [odb, lm2b, olrb], [ybfb[0]])
                                ys = [0]
                            else:
                                for cp in range(2):
                                    k.copy(ybf[cp][:], on[cp][:], [onb[cp]], [ybfb[cp]])
                                ys = [0, 1]
                            for yi, (oc, oh) in zip(ys, outs):
                                if oh == 0:
                                    k.copy(yTa[0:64, oc, qsl], ybf[yi][:], [ybfb[yi]], [yTab], e="act")
                                else:
                                    k.mm(ps[1][64:128, :], cb[0:64, CB_ID, 0:64], ybf[yi][:], [cbb, ybfb[yi]], [psb[1]])
                                    k.copy(yTa[64:128, oc, qsl], ps[1][64:128, :], [psb[1]], [yTab], e="act")
                    if stop == "attn" + phase_kind:
                        dump_bf(ph4, yTa[:, 0, :], [yTab], T)
                        dump_bf(ph4, yTa[:, 1, :], [yTab], T)
                        finish()
                        return nc
                    out_proj(ph4, yTa, yTab, "w_out", l, 512 if phase_kind == "d" else 768, 2, "t" + phase_kind, pre=wo_pre[phase_kind])
              k.barrier()
            if stop == "mix":
                for i in (0, 7, 15):
                    dump(x[:, i, :], [xb[i]], D)
                finish()
                return nc
            with ExitStack() as phC:
                hT = sb("hTc", [128, KC, T], BF16, phC)
                hTb = bufs(NT, "hTc")
                wq4 = [sb("wq", [128, KC, 256], BF16, phC) for _ in range(4)]
                wq4b = bufs(4)
                wk4 = [sb("wk", [128, KC, 256], BF16, phC) for _ in range(2)]
                wk4b = bufs(2)
                for h in range(2):
                    k.dma("pool", wk4[h][:], wv_("w_ckv", l, h * 256, (h + 1) * 256), [], [wk4b[h]], C(f"wk{h}"))
                for h in range(4):
                    k.dma("pool", wq4[h][:], wv_("w_cq", l, h * 256, (h + 1) * 256), [], [wq4b[h]], C(f"wq{h}"))
                wco_pre = [load_wo(phC, "w_co", l, h * 256, 2, f"c{h}") for h in range(4)]
                with ExitStack() as ph:
                    norm_T(ph, hT, hTb, PP_NCROSS, "c")
                memT = sb("memT", [128, KC, 256], BF16, phC)
                memTb = Buf()
                KT = sb("KTm", [128, 8, 256], BF16, phC)
                KTb = Buf()
                Vm = sb("Vm", [128, 2, D], BF16, phC)
                Vmb = Buf()
                with ExitStack() as ph:
                    mt_ = sb("mt", [128, 2, D], F32, ph)
                    mtb = Buf()
                    k.dma("sp", mt_[:], W["mem"]().rearrange("(i p) d -> p i d", p=128), [], [mtb], C("mt"))
                    ss2 = sb("ss2", [128, 2], F32, ph)
                    ss2b = Buf()
                    mj = sb("mj", [128, D], BF16, ph)
                    mjb = Buf()
                    for i in range(2):
                        k.act(mj[:], mt_[:, i, :], AF.Square, [mtb], [mjb, ss2b], accum_out=ss2[:, i:i + 1])
                    k.act(ss2[:], ss2[:], AF.Ln, [ss2b], [ss2b], scale=1.0 / D, bias=EPS)
                    k.act(ss2[:], ss2[:], AF.Exp, [ss2b], [ss2b], scale=-0.5)
                    for i in range(2):
                        k.ts(mj[:], mt_[:, i, :], ss2[:, i:i + 1], ALU.mult, [mtb, ss2b], [mjb])
                        pv = psbf(6 + i)
                        for kc in range(KC):
                            k.tr(pv[:, kc * 128:(kc + 1) * 128], mj[:, kc * 128:(kc + 1) * 128], ident, [mjb, cbb], [psb[6 + i]])
                        k.tt(memT[:, :, i * 128:(i + 1) * 128], pv.rearrange("p (k t) -> p k t", k=KC),
                             bcast(pp[:, PP_NMEM:PP_NMEM + KC].unsqueeze(2), [128, KC, 128]), ALU.mult, [psb[6 + i], ppb], [memTb])
                    wk, wkb = wk4, wk4b
                    sqk = sb("sqk", [128, 512], BF16, ph)
                    sqkb = Buf()
                    rk = sb("rk", [128, 256], F32, ph)
                    rkb = Buf()
                    for h in range(4):
                        a = h % 2
                        if h >= 2:
                            k.dma("pool", wk[a][:], wv_("w_ckv", l, h * 256, (h + 1) * 256), [], [wkb[a]], C(f"wk{a}"))
                        for c in range(2):
                            for kc in range(KC):
                                k.mm(ps[a][:, c * 256:(c + 1) * 256], wk[a][:, kc, c * 128:(c + 1) * 128], memT[:, kc, :],
                                     [wkb[a], memTb], [psb[a]], start=(kc == 0), stop=(kc == KC - 1))
                        k.act(sqk[:], ps[a][:, :], AF.Square, [psb[a]], [sqkb])
                        for c in range(2):
                            k.mm(ps[2][:, 0:256], cb[:, CB_ONES, :], sqk[:, c * 256:(c + 1) * 256], [cbb, sqkb], [psb[2]],
                                 start=(c == 0), stop=(c == 1))
                        k.act(rk[:], ps[2][:, 0:256], AF.Ln, [psb[2]], [rkb], scale=1.0 / 256, bias=EPS)
                        k.act(rk[:], rk[:], AF.Exp, [rkb], [rkb], scale=-0.5)
                        for c in range(2):
                            k.stt(KT[:, 2 * h + c, :], ps[a][:, c * 256:(c + 1) * 256], pp[:, PP_CKN + c:PP_CKN + c + 1], rk[:],
                                  ALU.mult, ALU.mult, [psb[a], ppb, rkb], [KTb])
                    wvv = sb("wvv", [128, KC, D], BF16, ph)
                    wvvb = Buf()
                    k.dma("pool", wvv[:], wv_("w_ckv", l, D, 2 * D), [], [wvvb], C("wvv"))
                    for i in range(2):
                        for half in range(2):
                            pb = 3 + half
                            for kc in range(KC):
                                k.mm(ps[pb][:, :], memT[:, kc, i * 128:(i + 1) * 128], wvv[:, kc, half * 512:(half + 1) * 512],
                                     [memTb, wvvb], [psb[pb]], start=(kc == 0), stop=(kc == KC - 1))
                            k.copy(Vm[:, i, half * 512:(half + 1) * 512], ps[pb][:, :], [psb[pb]], [Vmb], e="act")
                    k.barrier()
                with ExitStack() as ph:
                    wq, wqb = wq4, wq4b
                    sqq = sb("sqq", [128, 2, 512], BF16, ph)
                    sqqb = Buf()
                    rq = sb("rq", [128, 512], F32, ph)
                    rqb = Buf()
                    qn = sb("qn", [128, 2, 512], BF16, ph)
                    qnb = Buf()
                    ptc = [sb("ptc", [128, 512], BF16, ph) for _ in range(2)]
                    ptcb = bufs(2)
                    rcc = sb("rcc", [128, 512], F32, ph)
                    rccb = Buf()
                    coT = sb("coT", [128, 2, T], BF16, ph)
                    coTb = Buf()
                    for h in range(4):
                        a = h
                        def emit_qproj(tb):
                            tsl = slice(tb * 512, (tb + 1) * 512)
                            for c in range(2):
                                for kc in range(KC):
                                    k.mm(ps[c][:, :], wq[a][:, kc, c * 128:(c + 1) * 128], hT[:, kc, tsl],
                                         [wqb[a]] + hTb[4 * tb:4 * tb + 4], [psb[c]], start=(kc == 0), stop=(kc == KC - 1))
                                k.act(sqq[:, c, :], ps[c][:, :], AF.Square, [psb[c]], [sqqb])

                        def emit_qnorm():
                            for c in range(2):
                                k.mm(ps[2][:, :], cb[:, CB_ONES, :], sqq[:, c, :], [cbb, sqqb], [psb[2]], start=(c == 0), stop=(c == 1))
                            k.act(rq[:], ps[2][:, :], AF.Ln, [psb[2]], [rqb], scale=1.0 / 256, bias=EPS)
                            k.act(rq[:], rq[:], AF.Exp, [rqb], [rqb], scale=-0.5)
                            for c in range(2):
                                k.stt(qn[:, c, :], ps[c][:, :], pp[:, PP_CQN + c:PP_CQN + c + 1], rq[:], ALU.mult, ALU.mult,
                                      [psb[c], ppb, rqb], [qnb])

                        emit_qproj(0)
                        emit_qnorm()
                        for tb in range(4):
                            tsl = slice(tb * 512, (tb + 1) * 512)
                            for mi in range(2):
                                for c in range(2):
                                    k.mm(ps[3 + mi][:, :], KT[:, 2 * h + c, mi * 128:(mi + 1) * 128], qn[:, c, :], [KTb, qnb], [psb[3 + mi]],
                                         start=(c == 0), stop=(c == 1))
                                k.act(ptc[mi][:], ps[3 + mi][:, :], AF.Exp, [psb[3 + mi]], [ptcb[mi]], scale=1.0 / 16)
                            if tb + 1 < 4:
                                emit_qproj(tb + 1)
                                emit_qnorm()
                            for mi in range(2):
                                for dv in range(2):
                                    k.mm(ps[5 + dv][:, :], Vm[:, mi, h * 256 + dv * 128:h * 256 + (dv + 1) * 128], ptc[mi][:],
                                         [Vmb, ptcb[mi]], [psb[5 + dv]], start=(mi == 0), stop=(mi == 1))
                                k.mm(ps[7][:, :], cb[:, CB_ONES, :], ptc[mi][:], [cbb, ptcb[mi]], [psb[7]], start=(mi == 0), stop=(mi == 1))
                            k.op("dve", lambda g: g.reciprocal(rcc[:], ps[7][:, :]), [psb[7]], [rccb])
                            for dv in range(2):
                                k.tt(coT[:, dv, tsl], ps[5 + dv][:, :], rcc[:], ALU.mult, [psb[5 + dv], rccb], [coTb])
                        out_proj(ph, coT, coTb, "w_co", l, h * 256, 2, f"c{h}", pre=wco_pre[h])
                    k.barrier()
            if stop == "cross":
                for i in (0, 7, 15):
                    dump(x[:, i, :], [xb[i]], D)
                finish()
                return nc
            with ExitStack() as phF:
                hT = sb("hTf", [128, KC, T], BF16, phF)
                hTb = bufs(NT, "hTf")
                wfi2 = [sb("wfi", [128, KC, 2, 512], BF16, phF) for _ in range(2)]
                wfib2 = bufs(2)
                wfo2 = [sb("wfo", [128, 4, D], BF16, phF) for _ in range(2)]
                wfob2 = bufs(2)

                def load_ffn_group(grp):
                    ga_ = grp % 2
                    nch_ = 4 if grp < 5 else 2
                    k.dma("pool", wfi2[ga_][:, :, 0, 0:nch_ * 128], wv_("w_fi", l, grp * 512, grp * 512 + nch_ * 128), [], [wfib2[ga_]], C(f"wfi{ga_}"))
                    k.dma("pool", wfi2[ga_][:, :, 1, 0:nch_ * 128], wv_("w_fi", l, DFF + grp * 512, DFF + grp * 512 + nch_ * 128), [], [wfib2[ga_]], C(f"wfi{ga_}"))
                    k.dma("pool", wfo2[ga_][:, 0:nch_, :], W["w_fo"]()[l][grp * 512:grp * 512 + nch_ * 128, :].rearrange("(k p) c -> p k c", p=128),
                          [], [wfob2[ga_]], C(f"wfo{ga_}"))

                with ExitStack() as ph:
                    norm_T(ph, hT, hTb, PP_NFFN, "f")
                hx = sb("hx", [128, KC, 2], BF16, phF)
                hxb = Buf()
                with ExitStack() as ph:
                    he = sb("he", [128, 16], F32, ph)
                    heb = Buf()
                    k.copy(he[:, 0:8], hT[:, :, 0], [hTb[0]], [heb])
                    k.copy(he[:, 8:16], hT[:, :, T - 1], [hTb[NT - 1]], [heb])
                    k.dma("sp", exH, he[:], [heb], [exHb], C("exH"))
                    allgather(exH, gaH, exHb, gaHb, "agH")
                    load_ffn_group(0)
                    H4 = sb("H4", [128, 4, 16], F32, ph)
                    H4b = Buf()
                    k.dma("sp", H4[:], gaH.rearrange("(r p) c -> p r c", p=128), [gaHb], [H4b], C("H4"))
                    hs_ = sb("hs", [128, 2, 8], F32, ph)
                    hsb = Buf()
                    for side, (s0, c0) in enumerate(((4, 8), (8, 0))):
                        k.ts(hs_[:, side, :], H4[:, 0, c0:c0 + 8], sel[:, s0:s0 + 1], ALU.mult, [H4b, selb], [hsb])
                        for r in range(1, 4):
                            k.stt(hs_[:, side, :], H4[:, r, c0:c0 + 8], sel[:, s0 + r:s0 + r + 1], hs_[:, side, :], ALU.mult, ALU.add,
                                  [H4b, selb, hsb], [hsb])
                    k.copy(hx[:].rearrange("p k s -> p s k"), hs_[:], [hsb], [hxb])
                    k.barrier()
                with ExitStack() as ph:
                    aT = sb("aT", [128, 4, T], BF16, ph)
                    aTb = Buf()
                    gext = sb("gext", [128, T + 2], F32, ph)
                    gextb = Buf()
                    gc = sb("gc", [128, T], F32, ph)
                    gcb = Buf()
                    sg = sb("sg", [128, T], F32, ph)
                    sgb = Buf()
                    npb = 0
                    for grp in range(6):
                        nch = 4 if grp < 5 else 2
                        if grp + 1 < 6:
                            load_ffn_group(grp + 1)
                        wfi, wfib, wfo, wfob = wfi2[grp % 2], wfib2[grp % 2], wfo2[grp % 2], wfob2[grp % 2]
                        for fi in range(nch):
                            f = grp * 4 + fi
                            fsl = slice(fi * 128, (fi + 1) * 128)
                            for tb in range(4):
                                tsl = slice(tb * 512, (tb + 1) * 512)
                                pb = npb % 4
                                npb += 1
                                for kc in range(KC):
                                    k.mm(ps[pb][:, :], wfi[:, kc, 0, fsl], hT[:, kc, tsl], [wfib] + hTb[4 * tb:4 * tb + 4], [psb[pb]],
                                         start=(kc == 0), stop=(kc == KC - 1))
                                k.copy(gext[:, 1 + tb * 512:1 + (tb + 1) * 512], ps[pb][:, :], [psb[pb]], [gextb], e="act")
                            for kc in range(KC):
                                k.mm(ps[4][:, 0:2], wfi[:, kc, 0, fsl], hx[:, kc, :], [wfib, hxb], [psb[4]], start=(kc == 0), stop=(kc == KC - 1))
                            k.copy(gext[:, 0:1], ps[4][:, 0:1], [psb[4]], [gextb], e="act")
                            k.copy(gext[:, T + 1:T + 2], ps[4][:, 1:2], [psb[4]], [gextb], e="act")
                            cw = PP_CONV + 4 * f
                            k.ts(gc[:], gext[:, 1:T + 1], pp[:, cw + 1:cw + 2], ALU.mult, [gextb, ppb], [gcb], s2=pp[:, cw + 3:cw + 4], op1=ALU.add)
                            k.stt(gc[:], gext[:, 0:T], pp[:, cw:cw + 1], gc[:], ALU.mult, ALU.add, [gextb, ppb, gcb], [gcb])
                            k.stt(gc[:], gext[:, 2:T + 2], pp[:, cw + 2:cw + 3], gc[:], ALU.mult, ALU.add, [gextb, ppb, gcb], [gcb])
                            k.act(sg[:], gc[:], AF.Silu, [gcb], [sgb])
                            for tb in range(4):
                                tsl = slice(tb * 512, (tb + 1) * 512)
                                pb = npb % 4
                                npb += 1
                                for kc in range(KC):
                                    k.mm(ps[pb][:, :], wfi[:, kc, 1, fsl], hT[:, kc, tsl], [wfib] + hTb[4 * tb:4 * tb + 4], [psb[pb]],
                                         start=(kc == 0), stop=(kc == KC - 1))
                                k.tt(aT[:, fi, tsl], sg[:, tsl], ps[pb][:, :], ALU.mult, [sgb, psb[pb]], [aTb])
                        n = 0
                        for i in range(NT):
                            for half in range(2):
                                pb = 6 + (n % 2)
                                n += 1
                                for fi in range(nch):
                                    k.mm(ps[pb][:, :], aT[:, fi, i * 128:(i + 1) * 128], wfo[:, fi, half * 512:(half + 1) * 512],
                                         [aTb, wfob], [psb[pb]], start=(fi == 0), stop=(fi == nch - 1))
                                resid_add(i, half, pb)
                    k.barrier()
            if stop == "ffn" and l == 0:
                for i in (0, 7, 15):
                    dump(x[:, i, :], [xb[i]], D)
                finish()
                return nc
        yb_ = Buf()
        yv_ = y_d.rearrange("(i p) d -> p i d", p=128)
        for q in range(4):
            k.dma("sp", yv_[:, 4 * q:4 * q + 4, :], x[:, 4 * q:4 * q + 4, :], xb[4 * q:4 * q + 4], [yb_], C(f"y{q}"))
        finish()
    return nc


def _host_consts():
    cm = np.zeros((CM_N, 128, 128), np.float32)
    s = np.arange(128)[:, None]
    t = np.arange(128)[None, :]
    same = (s // 64) == (t // 64)
    cm[CM_MASKF] = (same & (s <= t))
    cm[CM_MASKB] = (same & (s >= t))
    cm[CM_CH0LO] = ((s < 64) & (t < 64))
    cm[CM_CH0HI] = ((s < 64) & (t >= 64))
    cm[CM_CH1LO] = ((s >= 64) & (t < 64))
    cm[CM_CH1HI] = ((s >= 64) & (t >= 64))
    cm[CM_CGE] = ((s // 64) >= (t // 64))
    cm[CM_CLE] = ((s // 64) <= (t // 64))
    cm[CM_ONES] = 1.0
    cb = np.zeros((CB_N, 128, 128), np.float32)
    cb[CB_ID] = np.eye(128)
    for m in range(128):
        if (m % 32) < 16:
            cb[CB_RPT, m + 16, m] = -1.0
        else:
            cb[CB_RPT, m - 16, m] = 1.0
    cb[CB_B32] = ((s // 32) == (t // 32))
    cb[CB_B64] = same
    cb[CB_ONES] = 1.0
    cb[CB_MASKF] = cm[CM_MASKF]
    cb[CB_MASKB] = cm[CM_MASKB]
    cb[CB_SEL, 64, 0:64] = 1.0
    return cm, cb


def _bands(j):
    bands = np.zeros((20, 128, 128), np.float32)
    s = np.arange(128)[:, None]
    t = np.arange(128)[None, :]
    for g, w in enumerate((2, 4, 8, 16)):
        half = w // 2
        lo = t - half
        hi = t + half - 1
        for which, off in enumerate((-128, 0, 128)):
            sg = s + off
            m = ((sg >= lo) & (sg <= hi)).astype(np.float32) / w
            if which == 1:
                m = m - (s == t)
            bands[3 * g + which] = m
        first = bands[3 * g + 1].copy()
        last = bands[3 * g + 1].copy()
        if j == 0:
            lo_c = np.maximum(lo, 0)
            cnt = (hi - lo_c + 1).astype(np.float32)
            first = ((s >= lo_c) & (s <= hi)).astype(np.float32) / cnt - (s == t)
        if j == 3:
            hi_c = np.minimum(hi, 127)
            cnt = (hi_c - lo + 1).astype(np.float32)
            last = ((s >= lo) & (s <= hi_c)).astype(np.float32) / cnt - (s == t)
        bands[12 + g] = first
        bands[16 + g] = last
    return bands


def _prep_inputs(inp):
    f = lambda a: np.ascontiguousarray(np.asarray(a, dtype=np.float32))
    cm, cb = _host_consts()
    pp = np.zeros((L, 128, PP_N), np.float32)
    bcv = np.zeros((L, 128, BC_N), np.float32)
    p = np.arange(128)
    for l in range(L):
        for base, key in ((PP_NMIX, "norm_mix"), (PP_NCROSS, "norm_cross"), (PP_NFFN, "norm_ffn"), (PP_NMEM, "norm_mem")):
            pp[l, :, base:base + 8] = f(inp[key])[l].reshape(8, 128).T
        pp[l, :, PP_DQ] = f(inp["diff_qnorm"])[l][p % 32]
        pp[l, :, PP_DK] = f(inp["diff_knorm"])[l][p % 32]
        pp[l, :, PP_GQ] = f(inp["gqa_qnorm"])[l][p % 64]
        pp[l, :, PP_GK] = f(inp["gqa_knorm"])[l][p % 64]
        pp[l, :, PP_DSUB] = f(inp["diff_subnorm"])[l][p % 64]
        pp[l, :, PP_PSC:PP_PSC + 2] = f(inp["pool_scale"])[l].reshape(2, 128).T
        pp[l, :, PP_CQN:PP_CQN + 2] = f(inp["cross_qnorm"])[l].reshape(2, 128).T
        pp[l, :, PP_CKN:PP_CKN + 2] = f(inp["cross_knorm"])[l].reshape(2, 128).T
        cw = f(inp["ffn_conv"])[l]
        cbias = f(inp["ffn_conv_b"])[l]
        for c in range(NFC):
            for j in range(3):
                pp[l, :, PP_CONV + 4 * c + j] = cw[j, c * 128:(c + 1) * 128]
            pp[l, :, PP_CONV + 4 * c + 3] = cbias[c * 128:(c + 1) * 128]
        bi = f(inp["ml_bias_i"])[l]
        bf_ = f(inp["ml_bias_f"])[l]
        gb = np.concatenate([bi[0], bf_[0], bi[1], bf_[1]])
        bcv[l, :, BC_GB:BC_GB + 16] = gb[None, :]
        bcv[l, :, BC_MLN:BC_MLN + 256] = f(inp["ml_norm"])[l][None, :]
        bcv[l, :, BC_LAM:BC_LAM + 128] = f(inp["diff_lambda"])[l].reshape(-1)[None, :]
    inv32 = (10000.0 ** (-np.arange(0, 32, 2, dtype=np.float32) / 32)).astype(np.float32)
    common = dict(w_in=f(inp["w_in"]), w_out=f(inp["w_out"]), w_cq=f(inp["w_cq"]), w_ckv=f(inp["w_ckv"]),
                  w_co=f(inp["w_co"]), w_ffn_in=f(inp["w_ffn_in"]), w_ffn_out=f(inp["w_ffn_out"]),
                  pool_w=f(inp["pool_w"]), pp=pp, bc=bcv, cmatb=cb)
    xs = f(inp["x"])
    mems = f(inp["mem"])
    maps = []
    for c in range(NCORES):
        b, j = c // 4, c % 4
        pos = np.arange(j * T, (j + 1) * T)
        d = np.arange(128)
        angD = pos[None, :].astype(np.float32) * inv32[(d % 32) % 16][:, None]
        rowp = (pos // 64).astype(np.float32)
        colp = (pos % 64).astype(np.float32)
        isrow = ((d % 64) < 32)[:, None]
        angG = np.where(isrow, rowp[None, :], colp[None, :]).astype(np.float32) * inv32[(d % 32) % 16][:, None]
        rope = np.stack([np.cos(angD), np.sin(angD), np.cos(angG), np.sin(angG)]).astype(np.float32)
        selv = np.zeros((128, 18), np.float32)
        for jj in range(4):
            selv[32 * jj:32 * jj + 32, 12 + jj] = 1.0
        for jj in range(2):
            selv[64 * jj:64 * jj + 64, 16 + jj] = 1.0
        selv[:, 0 + j] = 1.0
        if j > 0:
            selv[:, 4 + (j - 1)] = 1.0
        if j < 3:
            selv[:, 8 + (j + 1)] = 1.0
        cmc = np.concatenate([cm[0:9], _bands(j)], 0)
        m = dict(common)
        m.update(x=np.ascontiguousarray(xs[b, j * T:(j + 1) * T]), mem=np.ascontiguousarray(mems[b]),
                 sel=selv, cmat=cmc, rope=rope)
        maps.append(m)
    return maps


_NC_CACHE = {}


def kernel(**inputs):
    maps = _prep_inputs(inputs)
    if "nc" not in _NC_CACHE:
        _NC_CACHE["nc"] = build()
    nc = _NC_CACHE["nc"]
    maps = [{kk: vv for kk, vv in m.items() if kk in nc._declared} for m in maps]
    res = run_bass_kernel_spmd(nc, maps, core_ids=list(range(NCORES)))
    out = np.zeros((2, 4 * T, D), np.float32)
    for c in range(NCORES):
        out[c // 4, (c % 4) * T:(c % 4 + 1) * T] = res.results[c]["y"]
    return out
```

```python
import math
from contextlib import ExitStack
import numpy as np
import concourse.bass as bass
import concourse.mybir as mybir
from concourse.bass_utils import run_bass_kernel_spmd

F32 = mybir.dt.float32
BF16 = mybir.dt.bfloat16
AF = mybir.ActivationFunctionType
ALU = mybir.AluOpType
AX = mybir.AxisListType

NCORES = 8
T = 2048
NT = 16
D = 1024
KC = 8
DIN = 2576
DFF = 2816
NFC = 22
EPS = 1e-6
L = 2
RG = [[0, 1, 2, 3], [4, 5, 6, 7]]
LN8 = math.log(0.125)
SAME_ENGINE_SYNC = True
VW = 66
SW = 2 * VW
MS = 2 * SW

PP_NMIX, PP_NCROSS, PP_NFFN, PP_NMEM = 0, 8, 16, 24
PP_DQ, PP_DK, PP_GQ, PP_GK, PP_DSUB = 32, 33, 34, 35, 36
PP_PSC = 37
PP_CQN = 39
PP_CKN = 41
PP_CONV = 43
PP_N = 43 + 88
BC_GB, BC_MLN, BC_LAM = 0, 16, 272
BC_N = 272 + 128
CM_MASKF, CM_MASKB, CM_CH0LO, CM_CH0HI, CM_CH1LO, CM_CH1HI, CM_CGE, CM_CLE, CM_ONES = range(9)
CM_BAND = 9
CM_BFIRST = 21
CM_BLAST = 25
CM_N = 29
CB_ID, CB_RPT, CB_B32, CB_B64, CB_ONES, CB_MASKF, CB_MASKB, CB_SEL = range(8)
CB_N = 8


class Ctr:
    __slots__ = ("sem", "val", "nb")

    def __init__(self, sem):
        self.sem = sem
        self.val = 0
        self.nb = False


class Buf:
    __slots__ = ("w", "r", "name", "excl")

    def __init__(self, name="", excl=False):
        self.w = None
        self.r = []
        self.name = name
        self.excl = excl


def bufs(n, name=""):
    return [Buf(f"{name}{i}") for i in range(n)]


class KB:
    def __init__(self, nc, es):
        self.nc = nc
        self.es = es
        self.eng = {"pe": nc.tensor, "act": nc.scalar, "dve": nc.vector, "pool": nc.gpsimd, "sp": nc.sync}
        self.allctr = []
        self.ectr = {e: self.newctr("c_" + e) for e in self.eng}
        self.seen = {e: {} for e in self.eng}
        self.nins = 0

    def newctr(self, name):
        c = Ctr(self.es.enter_context(self.nc.semaphore(name)))
        self.allctr.append(c)
        return c

    def _wait(self, e, need):
        eng = self.eng[e]
        for c, v in need.items():
            if c is self.ectr.get(e) and (e == "pe" or not SAME_ENGINE_SYNC):
                continue
            if self.seen[e].get(c, 0) < v:
                eng.wait_ge(c.sem, v)
                self.seen[e][c] = v

    def op(self, e, fn, rd=(), wr=(), dma=None, inc=None):
        need = {}
        for b in rd:
            if b.w is not None:
                need[b.w[0]] = max(need.get(b.w[0], 0), b.w[1])
            if b.excl:
                for t in b.r:
                    if t[0] is not self.ectr.get(e):
                        need[t[0]] = max(need.get(t[0], 0), t[1])
        for b in wr:
            if b.w is not None:
                need[b.w[0]] = max(need.get(b.w[0], 0), b.w[1])
            for t in b.r:
                need[t[0]] = max(need.get(t[0], 0), t[1])
        self._wait(e, need)
        ins = fn(self.eng[e])
        self.nins += 1
        if dma is not None:
            c = dma
            step = 16 if inc is None else inc
        else:
            c = self.ectr[e]
            step = 1
        c.val += step
        ins.then_inc(c.sem, step)
        t = (c, c.val)
        for b in rd:
            b.r = [x for x in b.r if x[0] is not c] + [t]
        for b in wr:
            b.w = t
            b.r = []
        return ins

    def barrier(self, engines=("pe", "act", "dve", "pool", "sp")):
        need = {c: c.val for c in self.allctr if c.val > 0 and not c.nb}
        for e in engines:
            self._wait(e, need)

    def mm(self, out, lhsT, rhs, rd, wr, start=True, stop=True, skip=False):
        return self.op("pe", lambda g: g.matmul(out, lhsT, rhs, start=start, stop=stop, skip_group_check=skip), rd, wr)

    def tr(self, out, in_, ident, rd, wr):
        return self.op("pe", lambda g: g.transpose(out, in_, ident), rd, wr)

    def act(self, out, in_, func, rd, wr, bias=0.0, scale=1.0, accum_out=None, e="act"):
        if accum_out is not None:
            return self.op(e, lambda g: g.activation(out, in_, func, bias=bias, scale=scale, accum_out=accum_out), rd, wr)
        return self.op(e, lambda g: g.activation(out, in_, func, bias=bias, scale=scale), rd, wr)

    def tt(self, out, in0, in1, op, rd, wr, e="dve"):
        return self.op(e, lambda g: g.tensor_tensor(out, in0, in1, op), rd, wr)

    def ts(self, out, in0, s1, op0, rd, wr, s2=None, op1=None, e="dve"):
        if op1 is None:
            return self.op(e, lambda g: g.tensor_scalar(out, in0, s1, None, op0), rd, wr)
        return self.op(e, lambda g: g.tensor_scalar(out, in0, s1, s2, op0, op1), rd, wr)

    def stt(self, out, in0, scalar, in1, op0, op1, rd, wr):
        return self.op("dve", lambda g: g.scalar_tensor_tensor(out, in0, scalar, in1, op0, op1), rd, wr)

    def copy(self, out, in_, rd, wr, e="dve"):
        if e == "act":
            return self.op("act", lambda g: g.copy(out, in_), rd, wr)
        return self.op(e, lambda g: g.tensor_copy(out, in_), rd, wr)

    def dma(self, e, out, in_, rd, wr, ctr):
        return self.op(e, lambda g: g.dma_start(out=out, in_=in_), rd, wr, dma=ctr)


def bcast(ap, shape):
    return ap.to_broadcast(list(shape))


def build(stop="full", dbg_cols=0, nl=L, rg=None):
    global RG
    RG = rg if rg is not None else [[0, 1, 2, 3], [4, 5, 6, 7]]
    nc = bass.Bass("TRN2", target_bir_lowering=False)
    dt = nc.dram_tensor
    declared = {}

    def din(name, shape):
        if name not in declared:
            declared[name] = dt(name, shape, F32, kind="ExternalInput").ap()
        return declared[name]

    nc._declared = declared
    x_d = din("x", [T, D])
    W = {
        "w_in": lambda: din("w_in", [nl, D, DIN]), "w_out": lambda: din("w_out", [nl, D, D]),
        "w_cq": lambda: din("w_cq", [nl, D, D]), "w_ckv": lambda: din("w_ckv", [nl, D, 2 * D]),
        "w_co": lambda: din("w_co", [nl, D, D]), "w_fi": lambda: din("w_ffn_in", [nl, D, 2 * DFF]),
        "w_fo": lambda: din("w_ffn_out", [nl, DFF, D]), "pool_w": lambda: din("pool_w", [nl, 4, 64, 64]),
        "mem": lambda: din("mem", [256, D]),
    }
    pp_d = din("pp", [nl, 128, PP_N])
    bc_d = din("bc", [nl, 128, BC_N])
    sel_d = din("sel", [128, 18])
    cmat_d = din("cmat", [CM_N, 128, 128])
    cmatb_d = din("cmatb", [CB_N, 128, 128])
    rope_d = din("rope", [4, 128, T])
    y_d = dt("y", [T, D] if stop == "full" else [128, 8], F32, kind="ExternalOutput").ap()
    dbg_d = dt("dbg", [128, dbg_cols], F32, kind="ExternalOutput").ap() if dbg_cols else None
    exK = [dt(f"exK{c}", [128, T], BF16, kind="Internal").ap() for c in range(3)]
    gaK = [dt(f"gaK{c}", [4 * 128, T], BF16, kind="Internal").ap() for c in range(3)]
    exV = [dt(f"exV{c}", [T, 128], BF16, kind="Internal").ap() for c in range(3)]
    gaV = [dt(f"gaV{c}", [4 * T, 128], BF16, kind="Internal").ap() for c in range(3)]
    exM = dt("exM", [128, 528], F32, kind="Internal").ap()
    gaM = dt("gaM", [4 * 128, 528], F32, kind="Internal").ap()
    exH = dt("exH", [128, 16], F32, kind="Internal").ap()
    gaH = dt("gaH", [4 * 128, 16], F32, kind="Internal").ap()

    with ExitStack() as es:
        k = KB(nc, es)

        uid = [0]

        def sb(name, shape, dtype, st=es):
            uid[0] += 1
            return st.enter_context(nc.sbuf_tensor(f"s_{name}_{uid[0]}", shape, dtype))

        x = sb("x", [128, NT, D], F32)
        xb = bufs(NT, "x")
        cm = sb("cm", [128, 9, 128], F32)
        cmb_ = Buf("cm")
        cb = sb("cb", [128, CB_N, 128], BF16)
        cbb = Buf("cb")
        sel = sb("sel", [128, 18], F32)
        selb = Buf("sel")
        pp = sb("pp", [128, PP_N], F32)
        ppb = Buf("pp")
        bc = sb("bc", [128, BC_N], F32)
        bcb = Buf("bc")
        ps = [es.enter_context(nc.psum_tensor(f"ps{i}", [128, 512], F32)) for i in range(8)]
        psb = [Buf(f"ps{i}", excl=True) for i in range(8)]
        ld = k.newctr("ld0")

        def psbf(i):
            return ps[i][:].bitcast(BF16)

        xv = x_d.rearrange("(i p) d -> p i d", p=128)
        xc = [k.newctr(f"xld{q}") for q in range(4)]
        for q in range(4):
            k.dma("sp", x[:, 4 * q:4 * q + 4, :], xv[:, 4 * q:4 * q + 4, :], [], xb[4 * q:4 * q + 4], xc[q])
        k.dma("sp", cm[:], cmat_d[0:9].rearrange("m p c -> p m c"), [], [cmb_], ld)
        cbc = k.newctr("cbld")
        k.dma("pool", cb[:], cmatb_d.rearrange("m p c -> p m c"), [], [cbb], cbc)
        selc = k.newctr("selld")
        k.dma("sp", sel[:], sel_d, [], [selb], selc)
        ident = cb[:, CB_ID, :]

        dbg_state = {"col": 0}
        dbgc = k.newctr("dbg")

        def dump(ap2d, rd, ncols, parts=128):
            c0 = dbg_state["col"]
            k.dma("sp", dbg_d[0:parts, c0:c0 + ncols], ap2d, rd, [], dbgc)
            dbg_state["col"] = c0 + ncols

        def finish():
            k.barrier()

        norm_tmp = (sb("nss", [128, NT], F32), Buf(), sb("nln", [128, NT], F32), Buf(), sb("nrs", [128, NT], F32), Buf(),
                    [sb(f"njunk{j}", [128, D], BF16) for j in range(2)], bufs(2))

        def norm_T(ph, hT, hTb, gcol, tag):
            ss, ssb, lnv, lnb, rstd, rsb, junk, jb = norm_tmp
            for i in range(NT):
                k.act(junk[i % 2][:], x[:, i, :], AF.Square, [xb[i]], [jb[i % 2], ssb], accum_out=ss[:, i:i + 1])
            k.act(lnv[:], ss[:], AF.Ln, [ssb], [lnb], scale=1.0 / D, bias=EPS)
            k.act(rstd[:], lnv[:], AF.Exp, [lnb], [rsb], scale=-0.5)
            for i in range(NT):
                hn = junk[i % 2]
                k.ts(hn[:], x[:, i, :], rstd[:, i:i + 1], ALU.mult, [xb[i], rsb], [jb[i % 2]])
                pb = 6 + (i % 2)
                pv = psbf(pb)
                for kc in range(KC):
                    k.tr(pv[:, kc * 128:(kc + 1) * 128], hn[:, kc * 128:(kc + 1) * 128], ident, [jb[i % 2], cbb], [psb[pb]])
                k.tt(hT[:, :, i * 128:(i + 1) * 128], pv.rearrange("p (k t) -> p k t", k=KC),
                     bcast(pp[:, gcol:gcol + KC].unsqueeze(2), [128, KC, 128]), ALU.mult,
                     [psb[pb], ppb], [hTb[i]])

        ctrs = {}

        def C(name):
            if name not in ctrs:
                ctrs[name] = k.newctr(name)
            return ctrs[name]

        def wv_(name, l, c0, c1):
            return W[name]()[l].rearrange("(k p) c -> p k c", p=128)[:, :, c0:c1]

        def dump_bf(ph, ap2d, rdb, n):
            tmp = sb("dtmp", [128, 512], F32, ph)
            tb_ = Buf()
            for q in range(n // 512):
                k.copy(tmp[:], ap2d[:, q * 512:(q + 1) * 512], rdb, [tb_])
                dump(tmp[:], [tb_], 512)

        exKb, gaKb, exVb, gaVb = bufs(3), bufs(3), bufs(3), bufs(3)
        exMb, gaMb, exHb, gaHb = Buf(), Buf(), Buf(), Buf()

        def allgather(ex, ga, exb, gab, name):
            C(name).nb = True
            k.op("pool", lambda g: g.collective_compute("AllGather", ALU.bypass, replica_groups=RG, ins=[ex], outs=[ga]),
                 [exb], [gab], dma=C(name), inc=1)

        def resid_add(i, half, pbank, rd_extra=()):
            k.tt(x[:, i, half * 512:(half + 1) * 512], x[:, i, half * 512:(half + 1) * 512], ps[pbank][:, :], ALU.add,
                 [psb[pbank], xb[i]] + list(rd_extra), [xb[i]])

        def load_wo(ph, wname, l, r0, nk, tag):
            wo = sb("wo" + tag, [128, nk, D], BF16, ph)
            wob = Buf()
            k.dma("pool", wo[:], W[wname]()[l][r0:r0 + nk * 128, :].rearrange("(k p) c -> p k c", p=128), [], [wob], C("wo" + tag))
            return wo, wob

        def out_proj(ph, yT, yTb, wname, l, r0, nk, tag, pre=None):
            wo, wob = pre if pre is not None else load_wo(ph, wname, l, r0, nk, tag)
            n = 0
            for i in range(NT):
                for half in range(2):
                    pb = 6 + (n % 2)
                    n += 1
                    for kc in range(nk):
                        k.mm(ps[pb][:, :], yT[:, kc, i * 128:(i + 1) * 128], wo[:, kc, half * 512:(half + 1) * 512],
                             [yTb, wob], [psb[pb]], start=(kc == 0), stop=(kc == nk - 1))
                    resid_add(i, half, pb)

        for l in range(nl):
            k.dma("sp", pp[:], pp_d[l], [], [ppb], C("pp"))
            k.dma("sp", bc[:], bc_d[l], [], [bcb], C("bc"))
            lam_init = 0.8 - 0.6 * math.exp(-0.3 * l)
            with ExitStack() as phQ:
              qTd = sb("qTd", [128, 2, T], BF16, phQ)
              qTdb = Buf()
              qTg = sb("qTg", [128, 2, T], BF16, phQ)
              qTgb = Buf()
              phU = ExitStack()
              phQ.push(phU)
              u = sb("u", [128, NT, 256], F32, phU)
              ub = bufs(NT, "u")
              with ExitStack() as phA:
                hT = sb("hT", [128, KC, T], BF16, phA)
                hTb = bufs(NT, "hT")
                gp = sb("gp", [128, NT, 32], F32, phA)
                gpb = Buf()
                alp = sb("alp", [128, NT, 8], F32, phA)
                alpb = Buf()
                bet = sb("bet", [128, NT, 8], F32, phA)
                betb = Buf()
                gam = sb("gam", [128, NT, 8], F32, phA)
                gamb = Buf()
                with ExitStack() as ph1:
                    wqk = sb("wqk", [128, KC, 7 * 128], BF16, ph1)
                    wqkb = Buf()
                    wc = C("wqk")
                    k.dma("pool", wqk[:, :, 0:512], wv_("w_in", l, 1296, 1808), [], [wqkb], wc)
                    for (dst, src) in ((512, 2064), (576, 2192), (640, 2128), (704, 2256)):
                        k.dma("pool", wqk[:, :, dst:dst + 64], wv_("w_in", l, src, src + 64), [], [wqkb], wc)
                    k.dma("pool", wqk[:, :, 768:896], wv_("w_in", l, 2320, 2448), [], [wqkb], wc)
                    norm_T(ph1, hT, hTb, PP_NMIX, "a")
                    if stop == "norm1":
                        dump_bf(ph1, hT[:, 0, :], hTb, T)
                        finish()
                        return nc
                    rt = sb("rt", [128, 4, 512], F32, ph1)
                    rtb = Buf()
                    kst = sb("kst", [128, 3, T], BF16, ph1)
                    kstb = Buf()
                    sq = [sb("sq", [128, 512], BF16, ph1) for _ in range(2)]
                    lnv = [sb("lnq", [128, 512], F32, ph1) for _ in range(2)]
                    rs = [sb("rsq", [128, 512], F32, ph1) for _ in range(2)]
                    kn = [sb("kn", [128, 512], BF16, ph1) for _ in range(2)]
                    t1 = [sb("t1", [128, 512], F32, ph1) for _ in range(2)]
                    t2 = [sb("t2", [128, 512], F32, ph1) for _ in range(2)]
                    sqb, lnb, rsb, knb, t1b, t2b = bufs(2), bufs(2), bufs(2), bufs(2), bufs(2), bufs(2)
                    specs = [(qTd, qTdb, 0, PP_DQ, CB_B32, 32, 0), (qTd, qTdb, 1, PP_DQ, CB_B32, 32, 0),
                             (kst, kstb, 0, PP_DK, CB_B32, 32, 0), (kst, kstb, 1, PP_DK, CB_B32, 32, 0),
                             (qTg, qTgb, 0, PP_GQ, CB_B64, 64, 2), (qTg, qTgb, 1, PP_GQ, CB_B64, 64, 2),
                             (kst, kstb, 2, PP_GK, CB_B64, 64, 2)]
                    n = 0
                    for tb in range(4):
                        tsl = slice(tb * 512, (tb + 1) * 512)
                        k.dma("sp", rt[:], rope_d[:, :, tsl].rearrange("m p t -> p m t"), [], [rtb], C("rt"))
                        for ci, (dst, dstb, dc, gcol, blk, dd, tab) in enumerate(specs):
                            a = n % 2
                            n += 1
                            praw, pss, prot = ps[a], ps[2 + a], ps[4 + a]
                            for kc in range(KC):
                                k.mm(praw[:, :], wqk[:, kc, ci * 128:(ci + 1) * 128], hT[:, kc, tsl],
                                     [wqkb] + hTb[4 * tb:4 * tb + 4], [psb[a]], start=(kc == 0), stop=(kc == KC - 1))
                            lim = int(stop[2:]) if (stop.startswith("qk") and len(stop) > 2) else 99
                            def bail():
                                finish()
                                return nc
                            if lim <= 0:
                                return bail()
                            k.act(sq[a][:], praw[:, :], AF.Square, [psb[a]], [sqb[a]])
                            if lim <= 1:
                                return bail()
                            k.mm(pss[:, :], cb[:, blk, :], sq[a][:], [cbb, sqb[a]], [psb[2 + a]])
                            k.act(lnv[a][:], pss[:, :], AF.Ln, [psb[2 + a]], [lnb[a]], scale=1.0 / dd, bias=EPS)
                            k.act(rs[a][:], lnv[a][:], AF.Exp, [lnb[a]], [rsb[a]], scale=-0.5)
                            if lim <= 2:
                                return bail()
                            k.stt(kn[a][:], praw[:, :], pp[:, gcol:gcol + 1], rs[a][:], ALU.mult, ALU.mult,
                                  [psb[a], ppb, rsb[a]], [knb[a]])
                            if lim <= 3:
                                return bail()
                            k.mm(prot[:, :], cb[:, CB_RPT, :], kn[a][:], [cbb, knb[a]], [psb[4 + a]])
                            if lim <= 4:
                                return bail()
                            k.tt(t1[a][:], kn[a][:], rt[:, tab, :], ALU.mult, [knb[a], rtb], [t1b[a]], e="pool")
                            if lim <= 5:
                                return bail()
                            k.tt(t2[a][:], prot[:, :], rt[:, tab + 1, :], ALU.mult, [psb[4 + a], rtb], [t2b[a]])
                            k.tt(dst[:, dc, tsl], t1[a][:], t2[a][:], ALU.add, [t1b[a], t2b[a]], [dstb], e="pool")
                            if lim <= 6:
                                return bail()
                    for c in range(3):
                        k.dma("sp", exK[c], kst[:, c, :], [kstb], [exKb[c]], C(f"exK{c}"))
                    if stop == "qk":
                        dump_bf(ph1, qTd[:, 0, :], [qTdb], T)
                        dump_bf(ph1, qTg[:, 1, :], [qTgb], T)
                        dump_bf(ph1, kst[:, 2, :], [kstb], T)
                        finish()
                        return nc
                    k.barrier()
                wml = sb("wml", [128, KC, 1040], BF16, phA)
                wmlb = Buf()
                with ExitStack() as ph1:
                    wv = sb("wv", [128, KC, 384], BF16, ph1)
                    wvb = Buf()
                    k.dma("pool", wv[:, :, 0:256], wv_("w_in", l, 1808, 2064), [], [wvb], C("wv"))
                    k.dma("pool", wv[:, :, 256:384], wv_("w_in", l, 2448, 2576), [], [wvb], C("wv"))
                    k.dma("pool", wml[:], wv_("w_in", l, 0, 1040), [], [wmlb], C("wml"))
                    wpl = sb("wpl", [128, KC, 256], BF16, ph1)
                    wplb = Buf()
                    k.dma("pool", wpl[:], wv_("w_in", l, 1040, 1296), [], [wplb], C("wpl"))
                    vst = sb("vst", [128, NT, 384], BF16, ph1)
                    vstb = Buf()
                    for i in range(NT):
                        a = i % 2
                        for kc in range(KC):
                            k.mm(ps[a][:, 0:384], hT[:, kc, i * 128:(i + 1) * 128], wv[:, kc, :], [hTb[i], wvb], [psb[a]],
                                 start=(kc == 0), stop=(kc == KC - 1))
                        k.copy(vst[:, i, :], ps[a][:, 0:384], [psb[a]], [vstb], e="act")
                    for c in range(3):
                        k.dma("sp", exV[c].rearrange("(i p) c -> p i c", p=128), vst[:, :, c * 128:(c + 1) * 128],
                              [vstb], [exVb[c]], C(f"exV{c}"))
                    for i in range(NT):
                        a = 2 + i % 2
                        for kc in range(KC):
                            k.mm(ps[a][:, 0:256], hT[:, kc, i * 128:(i + 1) * 128], wpl[:, kc, :], [hTb[i], wplb], [psb[a]],
                                 start=(kc == 0), stop=(kc == KC - 1))
                        k.copy(u[:, i, :], ps[a][:, 0:256], [psb[a]], [ub[i]], e="act")
                    if stop == "p1a":
                        finish()
                        return nc
                    zt = [sb("zt", [128, 16], F32, ph1) for _ in range(2)]
                    ztb = bufs(2)
                    et = [sb("et", [128, 2, 4], F32, ph1) for _ in range(2)]
                    etb = bufs(2)
                    spA = sb("spA", [128, NT, 8], F32, ph1)
                    spb = bufs(NT)
                    liA = sb("liA", [128, NT, 8], F32, ph1)
                    liAb = Buf()
                    for i in range(NT):
                        a = i % 2
                        pg = ps[4 + a]
                        for kc in range(KC):
                            k.mm(pg[:, 0:16], hT[:, kc, i * 128:(i + 1) * 128], wml[:, kc, 1024:1040], [hTb[i], wmlb], [psb[4 + a]],
                                 start=(kc == 0), stop=(kc == KC - 1))
                        k.tt(zt[a][:], pg[:, 0:16], bc[:, BC_GB:BC_GB + 16], ALU.add, [psb[4 + a], bcb], [ztb[a]])
                        z4 = zt[a][:].rearrange("p (d k h) -> p d k h", d=2, k=2)
                        k.copy(liA[:, i, :].rearrange("p (d h) -> p d h", d=2), z4[:, :, 0, :], [ztb[a]], [liAb])
                        k.act(et[a][:], z4[:, :, 1, :], AF.Exp, [ztb[a]], [etb[a]], scale=-1.0)
                        sp3 = spA[:, i, :].rearrange("p (d h) -> p d h", d=2)
                        k.act(sp3, et[a][:], AF.Ln, [etb[a]], [spb[i]], bias=1.0)
                        pq = ps[6 + a]
                        sp4 = spA[:, i, :].rearrange("p (d a b) -> p d a b", d=2, a=2)
                        rdg = [spb[i], cmb_]
                        k.mm(pq[:, 0:4], cm[:, CM_MASKF, :], sp3[:, 0, :], rdg, [psb[6 + a]])
                        k.mm(pq[:, 4:8], cm[:, CM_MASKB, :], sp3[:, 1, :], rdg, [psb[6 + a]])
                        for c in range(2):
                            o4 = pq[:, 8 + 4 * c:12 + 4 * c].rearrange("p (d a) -> p d a", d=2)
                            k.mm(o4, cm[:, CM_CH0LO + 2 * c, :], sp4[:, :, :, 0], rdg, [psb[6 + a]], start=True, stop=False)
                            k.mm(o4, cm[:, CM_CH0HI + 2 * c, :], sp4[:, :, :, 1], rdg, [psb[6 + a]], start=False, stop=True)
                        k.mm(pq[:, 16:20], cm[:, CM_CGE, :], sp3[:, 0, :], rdg, [psb[6 + a]])
                        k.mm(pq[:, 20:24], cm[:, CM_CLE, :], sp3[:, 1, :], rdg, [psb[6 + a]])
                        k.mm(pq[:, 24:32], cm[:, CM_ONES, :], spA[:, i, :], rdg, [psb[6 + a]])
                        k.copy(gp[:, i, :], pq[:, 0:32], [psb[6 + a]], [gpb])
                    if stop == "p1b":
                        finish()
                        return nc
                    tmpb = sb("tmpb", [128, NT, 8], F32, ph1)
                    tmpbb = Buf()
                    wex = sb("wex", [128, NT, 8], F32, ph1)
                    wexb = Buf()
                    suf = sb("suf", [128, NT, 8], F32, ph1)
                    sufb = Buf()
                    k.act(alp[:], gp[:, :, 0:8], AF.Exp, [gpb], [alpb], scale=-1.0)
                    k.tt(tmpb[:], liA[:], gp[:, :, 0:8], ALU.add, [liAb, gpb], [tmpbb])
                    k.act(bet[:], tmpb[:], AF.Exp, [tmpbb], [betb], bias=LN8)
                    k.act(gam[:], gp[:, :, 8:16], AF.Exp, [gpb], [gamb], scale=-1.0)
                    if stop == "p1c1":
                        finish()
                        return nc
                    k.op("pool", lambda g: g.memset(suf[:], 0.0), [], [sufb])
                    for i in range(NT - 2, -1, -1):
                        k.tt(suf[:, i, 0:4], suf[:, i + 1, 0:4], gp[:, i + 1, 24:28], ALU.add, [sufb, gpb], [sufb], e="pool")
                    for i in range(1, NT):
                        k.tt(suf[:, i, 4:8], suf[:, i - 1, 4:8], gp[:, i - 1, 28:32], ALU.add, [sufb, gpb], [sufb], e="pool")
                    if stop == "p1c2":
                        finish()
                        return nc
                    k.tt(wex[:], tmpb[:], gp[:, :, 16:24], ALU.subtract, [tmpbb, gpb], [wexb])
                    k.tt(wex[:], wex[:], suf[:], ALU.subtract, [wexb, sufb], [wexb])
                    k.act(wex[:], wex[:], AF.Exp, [wexb], [wexb], bias=LN8)
                    if stop == "p1c3":
                        finish()
                        return nc
                    msum = sb("msum", [128, MS + 4], F32, ph1)
                    msumb = Buf()
                    tot = sb("tot", [128, 8], F32, ph1)
                    totb = Buf()
                    k.tt(tot[:, 0:4], suf[:, 0, 0:4], gp[:, 0, 24:28], ALU.add, [sufb, gpb], [totb])
                    k.tt(tot[:, 4:8], suf[:, NT - 1, 4:8], gp[:, NT - 1, 28:32], ALU.add, [sufb, gpb], [totb])
                    tot4 = tot[:].rearrange("p (d a b) -> p d a b", d=2, a=2)
                    m3 = msum[:, MS:MS + 4].rearrange("p (d a) -> p d a", d=2)
                    k.copy(m3[0:64], tot4[0:64, :, :, 0], [totb], [msumb])
                    k.copy(m3[64:128], tot4[64:128, :, :, 1], [totb], [msumb])
                    if stop == "p1c":
                        finish()
                        return nc
                    kw = [[sb("kw", [128, 4, 64], BF16, ph1) for _ in range(2)] for _ in range(2)]
                    kwb = [bufs(2), bufs(2)]
                    vx = [sb("vx", [128, 4, VW], BF16, ph1) for _ in range(2)]
                    vxb = bufs(2)
                    for a in range(2):
                        k.op("pool", lambda g, a=a: g.memset(vx[a][:], 1.0), [], [vxb[a]])
                    psum_b = 0
                    k.op("pool", lambda g: g.memset(msum[:, 0:MS], 0.0), [], [msumb])
                    for i in range(NT):
                        first = {0: True, 1: True}
                        a = 0 if stop == "s4z" else i % 2
                        pk = ps[2 + a]
                        for kc in range(KC):
                            k.mm(pk[:, :], hT[:, kc, i * 128:(i + 1) * 128], wml[:, kc, 256:768], [hTb[i], wmlb], [psb[2 + a]],
                                 start=(kc == 0), stop=(kc == KC - 1))
                        if stop == "s1" and i == 1:
                            finish()
                            return nc
                        k4 = pk[:, 0:256].rearrange("p (h d) -> p h d", h=4)
                        for dr in range(2):
                            k.tt(kw[dr][a][:], k4, bcast(wex[:, i, 4 * dr:4 * dr + 4].unsqueeze(2), [128, 4, 64]), ALU.mult,
                                 [psb[2 + a], wexb], [kwb[dr][a]])
                        if stop == "s2" and i == 1:
                            finish()
                            return nc
                        k.copy(vx[a][:, :, 0:64], pk[:, 256:512].rearrange("p (h d) -> p h d", h=4), [psb[2 + a], kwb[0][a], kwb[1][a]], [vxb[a]], e="act")
                        if stop == "s3" and i == 1:
                            finish()
                            return nc
                        for dr in range(2):
                            for h in range(4):
                                half = h % 2
                                st_ = first[half]
                                first[half] = False
                                c0 = dr * SW + (h // 2) * VW
                                k.mm(ps[psum_b][half * 64:(half + 1) * 64, c0:c0 + VW], kw[dr][a][:, h, :], vx[a][:, h, :],
                                     [kwb[dr][a], vxb[a]], [psb[psum_b]], start=st_, stop=(dr == 1 and h >= 2), skip=True)
                        k.tt(msum[:, 0:MS], msum[:, 0:MS], ps[psum_b][:, 0:MS], ALU.add, [msumb, psb[psum_b]], [msumb])
                        if stop in ("s4", "s4z") and i == 1:
                            finish()
                            return nc
                    if stop == "s5":
                        finish()
                        return nc
                    if stop == "s6":
                        finish()
                        return nc
                    if stop == "p1d1":
                        dump(msum[:], [msumb], MS + 4)
                        finish()
                        return nc
                    k.dma("sp", exM[:, 0:MS + 4], msum[:], [msumb], [exMb], C("exM"))
                    k.dma("sp", exM[0:8, MS + 4:MS + 260], u[0:8, 0, :], [ub[0]], [exMb], C("exM"))
                    k.dma("sp", exM[120:128, MS + 4:MS + 260], u[120:128, NT - 1, :], [ub[NT - 1]], [exMb], C("exM"))
                    if stop == "p1d":
                        finish()
                        return nc
                    allgather(exM, gaM, exMb, gaMb, "agM")
                    for c in range(3):
                        allgather(exK[c], gaK[c], exKb[c], gaKb[c], f"agK{c}")
                    for c in range(3):
                        allgather(exV[c], gaV[c], exVb[c], gaVb[c], f"agV{c}")
                    if stop == "p1":
                        dump(gp[:].rearrange("p i c -> p (i c)"), [gpb], 512)
                        dump(msum[:], [msumb], MS + 4)
                        dump(wex[:].rearrange("p i c -> p (i c)"), [wexb], 128)
                        finish()
                        return nc
                    k.barrier()
                if stop == "m0":
                    finish()
                    return nc
                with ExitStack() as ph2:
                    yT = sb("yT", [128, 2, T], BF16, ph2)
                    yTb = Buf()
                    C0 = sb("C0", [128, 2, SW], F32, ph2)
                    C0b = Buf()
                    phM = ExitStack()
                    ph2.push(phM)
                    M4 = sb("M4", [128, 4, MS + 4], F32, phM)
                    M4b = Buf()
                    k.dma("sp", M4[:], gaM.rearrange("(r p) c -> p r c", p=128)[:, :, 0:MS + 4], [gaMb], [M4b], C("M4"))
                    E4 = sb("E4", [128, 4, 4], F32, phM)
                    E4b = Buf()
                    k.act(E4[:], M4[:, :, MS:MS + 4], AF.Exp, [M4b], [E4b], scale=-1.0)
                    acc = sb("acc", [128, SW], F32, phM)
                    accb = Buf()
                    k.op("dve", lambda g: g.memset(C0[:], 0.0), [], [C0b])
                    for dr in range(2):
                        k.op("dve", lambda g: g.memset(acc[:], 0.0), [], [accb])
                        order = range(4) if dr == 0 else range(3, -1, -1)
                        for r in order:
                            k.stt(C0[:, dr, :], acc[:], sel[:, r:r + 1], C0[:, dr, :], ALU.mult, ALU.add, [accb, selb, C0b], [C0b])
                            a3 = acc[:].rearrange("p (a e) -> p a e", a=2)
                            k.tt(a3, a3, bcast(E4[:, r, 2 * dr:2 * dr + 2].unsqueeze(2), [128, 2, VW]), ALU.mult, [accb, E4b], [accb])
                            k.tt(acc[:], acc[:], M4[:, r, dr * SW:(dr + 1) * SW], ALU.add, [accb, M4b], [accb])
                    k.barrier()
                    phM.close()
                    if stop == "m1":
                        finish()
                        return nc
                    CbA = sb("CbA", [128, NT, 2, SW], BF16, ph2)
                    CbAb = Buf()
                    Sb = sb("Sb", [128, SW], F32, ph2)
                    Sbb = Buf()
                    St = sb("St", [128, SW], F32, ph2)
                    Stb = Buf()
                    betm = sb("betm", [128, NT, 2, 8], F32, ph2)
                    betmb = Buf()
                    for c in range(2):
                        k.ts(betm[:, :, c, :], bet[:], cm[:, CM_CH0LO + 2 * c, 0:1], ALU.mult, [betb, cmb_], [betmb])
                    ks = [[sb("ks", [128, 4, 64], BF16, ph2) for _ in range(2)] for _ in range(2)]
                    ksb = [bufs(2), bufs(2)]
                    vx = [sb("vx2", [128, 4, VW], BF16, ph2) for _ in range(2)]
                    vxb = bufs(2)
                    for a in range(2):
                        k.op("dve", lambda g, a=a: g.memset(vx[a][:], 1.0), [], [vxb[a]])
                    k.copy(Sb[:], C0[:, 1, :], [C0b], [Sbb])
                    for i in range(NT - 1, -1, -1):
                        a = i % 2
                        pk = ps[a]
                        for kc in range(KC):
                            k.mm(pk[:, :], hT[:, kc, i * 128:(i + 1) * 128], wml[:, kc, 256:768], [hTb[i], wmlb], [psb[a]],
                                 start=(kc == 0), stop=(kc == KC - 1))
                        for c in range(2):
                            k.tt(ks[a][c][:], pk[:, 0:256].rearrange("p (h d) -> p h d", h=4),
                                 bcast(betm[:, i, c, 4:8].unsqueeze(2), [128, 4, 64]), ALU.mult, [psb[a], betmb], [ksb[a][c]])
                        k.copy(vx[a][:, :, 0:64], pk[:, 256:512].rearrange("p (h d) -> p h d", h=4), [psb[a]], [vxb[a]], e="act")
                        pd = ps[2 + a]
                        for c in range(2):
                            for h in range(4):
                                half = h % 2
                                c0 = c * SW + (h // 2) * VW
                                k.mm(pd[half * 64:(half + 1) * 64, c0:c0 + VW], ks[a][c][:, h, :],
                                     vx[a][:, h, :], [ksb[a][c], vxb[a]], [psb[2 + a]])
                        if stop == "m2a":
                            finish()
                            return nc
                        for c in (1, 0):
                            if stop == "m2b" and c == 0:
                                finish()
                                return nc
                            k.copy(CbA[:, i, c, :], Sb[:], [Sbb], [CbAb], e="act")
                            k.tt(St[:], Sb[:], pd[:, c * SW:(c + 1) * SW], ALU.add, [Sbb, psb[2 + a]], [Stb])
                            gidx = c * 4 + 2
                            k.tt(Sb[:].rearrange("p (a e) -> p a e", a=2), St[:].rearrange("p (a e) -> p a e", a=2),
                                 bcast(gam[:, i, gidx:gidx + 2].unsqueeze(2), [128, 2, VW]), ALU.mult, [Stb, gamb], [Sbb])
                    if stop == "m2":
                        finish()
                        return nc
                    Sf = sb("Sf", [128, SW], F32, ph2)
                    Sfb = Buf()
                    Cz = [[[sb("Cz", [128, SW], BF16, ph2) for _ in range(2)] for _ in range(2)] for _ in range(2)]
                    Czb = [[bufs(2), bufs(2)], [bufs(2), bufs(2)]]
                    ksm = [[sb("ksm", [128, 4, 64], BF16, ph2) for _ in range(2)] for _ in range(2)]
                    ksmb = [bufs(2), bufs(2)]
                    k.copy(Sf[:], C0[:, 0, :], [C0b], [Sfb])
                    qs = [[sb("qs", [128, 4, 64], BF16, ph2) for _ in range(2)] for _ in range(2)]
                    qsb = [bufs(2), bufs(2)]
                    ks2 = [[sb("ks2", [128, 4, 64], BF16, ph2) for _ in range(2)] for _ in range(2)]
                    ks2b = [bufs(2), bufs(2)]
                    T8 = [sb("T8", [128, 8, 128], BF16, ph2)] * 2
                    T8b = [Buf()] * 2
                    Am = [[sb("Am", [128, 2, 2, 128], BF16, ph2) for _ in range(2)]] * 2
                    Amb = [bufs(2)] * 2
                    eo = [sb("eo", [128, 256], F32, ph2) for _ in range(2)]
                    eob = bufs(2)
                    den = sb("den", [128, 2, 4], F32, ph2)
                    denb = Buf()
                    hh = [sb("hh", [128, 4, 64], F32, ph2) for _ in range(2)]
                    hhb = bufs(2)
                    hsq = sb("hsq", [128, 4, 64], F32, ph2)
                    hsqb = Buf()
                    ssm = sb("ssm", [128, 4], F32, ph2)
                    ssmb = Buf()
                    yv = sb("yv", [128, 256], F32, ph2)
                    yvb = Buf()
                    yb16 = sb("yb16", [128, 256], BF16, ph2)
                    yb16b = Buf()
                    pqk, pvo = ps[0], ps[1]

                    def emit_proj(i):
                        for kc in range(KC):
                            k.mm(pqk[:, :], hT[:, kc, i * 128:(i + 1) * 128], wml[:, kc, 0:512], [hTb[i], wmlb], [psb[0]],
                                 start=(kc == 0), stop=(kc == KC - 1))
                        for kc in range(KC):
                            k.mm(pvo[:, :], hT[:, kc, i * 128:(i + 1) * 128], wml[:, kc, 512:1024], [hTb[i], wmlb], [psb[1]],
                                 start=(kc == 0), stop=(kc == KC - 1))

                    emit_proj(0)
                    for i in range(NT):
                        a = i % 2
                        q4 = pqk[:, 0:256].rearrange("p (h d) -> p h d", h=4)
                        k4 = pqk[:, 256:512].rearrange("p (h d) -> p h d", h=4)
                        for dr in range(2):
                            k.tt(qs[dr][a][:], q4, bcast(alp[:, i, 4 * dr:4 * dr + 4].unsqueeze(2), [128, 4, 64]), ALU.mult,
                                 [psb[0], alpb], [qsb[dr][a]])
                            k.tt(ks2[dr][a][:], k4, bcast(bet[:, i, 4 * dr:4 * dr + 4].unsqueeze(2), [128, 4, 64]), ALU.mult,
                                 [psb[0], betb], [ks2b[dr][a]])
                        for c in range(2):
                            k.tt(ksm[a][c][:], k4, bcast(betm[:, i, c, 0:4].unsqueeze(2), [128, 4, 64]), ALU.mult,
                                 [psb[0], betmb], [ksmb[a][c]])
                        k.copy(vx[a][:, :, 0:64], pvo[:, 0:256].rearrange("p (h d) -> p h d", h=4), [psb[1]], [vxb[a]], e="act")
                        k.act(eo[a][:], pvo[:, 256:512], AF.Exp, [psb[1]], [eob[a]], scale=-1.0)
                        pT = psbf(2)
                        srcs = [(qs[0][a], qsb[0][a]), (ks2[0][a], ks2b[0][a]), (qs[1][a], qsb[1][a]), (ks2[1][a], ks2b[1][a])]
                        for si, (sa, sab) in enumerate(srcs):
                            for pr in range(2):
                                j = si * 2 + pr
                                k.tr(pT[:, j * 128:(j + 1) * 128], sa[:, 2 * pr:2 * pr + 2, :].rearrange("p h d -> p (h d)"), ident,
                                     [sab, cbb], [psb[2]])
                        k.copy(T8[a][:].rearrange("p j t -> p (j t)"), pT[:, 0:1024], [psb[2]], [T8b[a]])
                        for half in range(2):
                            pA = ps[3 + half]
                            hs_ = slice(half * 64, half * 64 + 64)
                            for dr in range(2):
                                for pr in range(2):
                                    cc = (dr * 2 + pr) * 128
                                    k.mm(pA[:, cc:cc + 128], T8[a][hs_, 4 * dr + 2 + pr, :], T8[a][hs_, 4 * dr + pr, :],
                                         [T8b[a]], [psb[3 + half]])
                            k.tt(Am[a][half][:], pA[:, :].rearrange("p (d j t) -> p d j t", d=2, j=2),
                                 bcast(cm[:, CM_MASKF:CM_MASKF + 2, :].unsqueeze(2), [128, 2, 2, 128]), ALU.mult,
                                 [psb[3 + half], cmb_], [Amb[a][half]])
                        pd = ps[5]
                        for c in range(2):
                            for h in range(4):
                                half = h % 2
                                c0 = c * SW + (h // 2) * VW
                                k.mm(pd[half * 64:(half + 1) * 64, c0:c0 + VW], ksm[a][c][:, h, :], vx[a][:, h, :],
                                     [ksmb[a][c], vxb[a]], [psb[5]])
                        if i + 1 < NT:
                            emit_proj(i + 1)
                        for c in range(2):
                            for half in range(2):
                                k.ts(Cz[0][c][half][:], Sf[:], cm[:, CM_CH0LO + 2 * half, 0:1], ALU.mult, [Sfb, cmb_], [Czb[0][c][half]])
                                k.ts(Cz[1][c][half][:], CbA[:, i, c, :], cm[:, CM_CH0LO + 2 * half, 0:1], ALU.mult, [CbAb, cmb_], [Czb[1][c][half]])
                            k.tt(St[:], Sf[:], pd[:, c * SW:(c + 1) * SW], ALU.add, [Sfb, psb[5]], [Stb])
                            gidx = c * 4
                            k.tt(Sf[:].rearrange("p (a e) -> p a e", a=2), St[:].rearrange("p (a e) -> p a e", a=2),
                                 bcast(gam[:, i, gidx:gidx + 2].unsqueeze(2), [128, 2, VW]), ALU.mult, [Stb, gamb], [Sfb])
                        for dr in range(2):
                            po = ps[6 + dr]
                            for h in range(4):
                                half = h % 2
                                k.mm(po[:, h * VW:(h + 1) * VW], Am[a][half][:, dr, h // 2, :], vx[a][:, h, :], [Amb[a][half], vxb[a]], [psb[6 + dr]],
                                     start=True, stop=False)
                                for c in range(2):
                                    k.mm(po[c * 64:(c + 1) * 64, h * VW:(h + 1) * VW], T8[a][:, 4 * dr + h // 2, c * 64:(c + 1) * 64],
                                         Cz[dr][c][half][:, (h // 2) * VW:(h // 2) * VW + VW],
                                         [T8b[a], Czb[dr][c][half]], [psb[6 + dr]], start=False, stop=(c == 1), skip=True)
                            po3 = po[:, 0:4 * VW].rearrange("p (h e) -> p h e", h=4)
                            k.ts(den[:, dr, :], po3[:, :, 64], -1.0, ALU.mult, [psb[6 + dr]], [denb])
                            k.stt(den[:, dr, :], po3[:, :, 64], 1.0, den[:, dr, :], ALU.max, ALU.max, [psb[6 + dr], denb], [denb])
                            k.op("dve", lambda g, dr=dr: g.reciprocal(den[:, dr, :], den[:, dr, :]), [denb], [denb])
                            k.tt(hh[dr][:], po3[:, :, 0:64], bcast(den[:, dr, :].unsqueeze(2), [128, 4, 64]), ALU.mult,
                                 [psb[6 + dr], denb], [hhb[dr]])
                        k.tt(hh[0][:], hh[0][:], hh[1][:], ALU.add, [hhb[0], hhb[1]], [hhb[0]])
                        k.tt(hsq[:], hh[0][:], hh[0][:], ALU.mult, [hhb[0]], [hsqb])
                        k.op("dve", lambda g: g.tensor_reduce(ssm[:], hsq[:], AX.X, ALU.add), [hsqb], [ssmb])
                        k.act(ssm[:], ssm[:], AF.Ln, [ssmb], [ssmb], scale=1.0 / 64, bias=EPS)
                        k.act(ssm[:], ssm[:], AF.Exp, [ssmb], [ssmb], scale=-0.5)
                        k.tt(hh[0][:], hh[0][:], bcast(ssm[:].unsqueeze(2), [128, 4, 64]), ALU.mult, [hhb[0], ssmb], [hhb[0]])
                        k.tt(yv[:], hh[0][:].rearrange("p h d -> p (h d)"), bc[:, BC_MLN:BC_MLN + 256], ALU.mult, [hhb[0], bcb], [yvb])
                        k.ts(eo[a][:], eo[a][:], 1.0, ALU.add, [eob[a]], [eob[a]])
                        k.op("dve", lambda g, a=a: g.reciprocal(eo[a][:], eo[a][:]), [eob[a]], [eob[a]])
                        k.tt(yb16[:], yv[:], eo[a][:], ALU.mult, [yvb, eob[a]], [yb16b])
                        pT2 = psbf(2)
                        for pr in range(2):
                            k.tr(pT2[:, pr * 128:(pr + 1) * 128], yb16[:, pr * 128:(pr + 1) * 128], ident, [yb16b, cbb], [psb[2]])
                        k.copy(yT[:, :, i * 128:(i + 1) * 128], pT2[:, 0:256].rearrange("p (j t) -> p j t", j=2), [psb[2]], [yTb], e="act")
                    if stop == "ml":
                        dump_bf(ph2, yT[:, 0, :], [yTb], T)
                        dump_bf(ph2, yT[:, 1, :], [yTb], T)
                        finish()
                        return nc
                    out_proj(ph2, yT, yTb, "w_out", l, 0, 2, "a")
                    k.barrier()
              k.barrier()
              with ExitStack() as ph3:
                cm2 = sb("cm2", [128, 20, 128], F32, ph3)
                cm2b = Buf()
                k.dma("sp", cm2[:], cmat_d[9:29].rearrange("m p c -> p m c"), [], [cm2b], C("cm2"))
                pw = sb("pw", [64, 4, 64], F32, ph3)
                pwb = Buf()
                k.dma("sp", pw[:], W["pool_w"]()[l].rearrange("g c e -> c g e"), [], [pwb], C("pw"))
                HL = sb("HL", [128, 4, 256], F32, ph3)
                HR = sb("HR", [128, 4, 256], F32, ph3)
                HLb, HRb = Buf(), Buf()
                k.op("pool", lambda g: g.memset(HL[:], 0.0), [], [HLb])
                k.op("pool", lambda g: g.memset(HR[:], 0.0), [], [HRb])
                gM3 = gaM.rearrange("(r p) c -> p r c", p=128)
                k.dma("sp", HL[120:128, :, :], gM3[120:128, :, MS + 4:MS + 260], [gaMb], [HLb], C("HL"))
                k.dma("sp", HR[0:8, :, :], gM3[0:8, :, MS + 4:MS + 260], [gaMb], [HRb], C("HR"))
                uL = sb("uL", [128, 256], F32, ph3)
                uR = sb("uR", [128, 256], F32, ph3)
                uLb, uRb = Buf(), Buf()
                for (dst, dstb, src, srcb, s0) in ((uL, uLb, HL, HLb, 4), (uR, uRb, HR, HRb, 8)):
                    k.ts(dst[:], src[:, 0, :], sel[:, s0:s0 + 1], ALU.mult, [srcb, selb], [dstb])
                    for r in range(1, 4):
                        k.stt(dst[:], src[:, r, :], sel[:, s0 + r:s0 + r + 1], dst[:], ALU.mult, ALU.add, [srcb, selb, dstb], [dstb])
                yTp = sb("yTp", [128, 2, T], BF16, ph3)
                yTpb = Buf()
                pl = [sb("pl", [64, 4, 128], F32, ph3) for _ in range(2)]
                plb = bufs(2)
                for i in range(NT):
                    a = i % 2
                    pp_ = ps[a]
                    for g_ in range(4):
                        gs = slice(g_ * 64, (g_ + 1) * 64)
                        up = (uL[:, gs], uLb) if i == 0 else (u[:, i - 1, gs], ub[i - 1])
                        un = (uR[:, gs], uRb) if i == NT - 1 else (u[:, i + 1, gs], ub[i + 1])
                        if i == 0:
                            bcur = cm2[:, 12 + g_, :]
                        elif i == NT - 1:
                            bcur = cm2[:, 16 + g_, :]
                        else:
                            bcur = cm2[:, 3 * g_ + 1, :]
                        o_ = pp_[0:64, g_ * 128:(g_ + 1) * 128]
                        k.mm(o_, up[0], cm2[:, 3 * g_ + 0, :], [up[1], cm2b], [psb[a]], start=True, stop=False)
                        k.mm(o_, u[:, i, gs], bcur, [ub[i], cm2b], [psb[a]], start=False, stop=False)
                        k.mm(o_, un[0], cm2[:, 3 * g_ + 2, :], [un[1], cm2b], [psb[a]], start=False, stop=True)
                    k.copy(pl[a][:].rearrange("p g t -> p (g t)"), pp_[0:64, :], [psb[a]], [plb[a]], e="act")
                    py = ps[2 + a]
                    for g_ in range(4):
                        k.mm(py[(g_ % 2) * 64:(g_ % 2) * 64 + 64, (g_ // 2) * 128:(g_ // 2) * 128 + 128], pw[:, g_, :], pl[a][:, g_, :],
                             [pwb, plb[a]], [psb[2 + a]])
                    k.tt(yTp[:, :, i * 128:(i + 1) * 128], py[:, 0:256].rearrange("p (j t) -> p j t", j=2),
                         bcast(pp[:, PP_PSC:PP_PSC + 2].unsqueeze(2), [128, 2, 128]), ALU.mult, [psb[2 + a], ppb], [yTpb])
                if stop == "pool":
                    dump_bf(ph3, yTp[:, 0, :], [yTpb], T)
                    dump_bf(ph3, yTp[:, 1, :], [yTpb], T)
                    finish()
                    return nc
                out_proj(ph3, yTp, yTpb, "w_out", l, 256, 2, "p")
                k.barrier()
              phU.close()
              with ExitStack() as ph4:
                wo_pre = {"d": load_wo(ph4, "w_out", l, 512, 2, "td"), "g": load_wo(ph4, "w_out", l, 768, 2, "tg")}
                Kb = [sb("Kb", [128, 4 * T], BF16, ph4) for _ in range(2)]
                Kbb = bufs(2)
                Vb = [sb("Vb", [128, 64, VW], BF16, ph4) for _ in range(4)]
                Vbb = bufs(4)
                for a in range(4):
                    k.op("pool", lambda g, a=a: g.memset(Vb[a][:], 1.0), [], [Vbb[a]])
                pt = [sb("pt", [128, 512], BF16, ph4) for _ in range(6)]
                ptb = bufs(6)
                Ohl = [sb("Ohl", [VW, 2, 512], BF16, ph4) for _ in range(2)]
                Ohlb = bufs(2)
                for a in range(2):
                    k.op("dve", lambda g, a=a: g.memset(Ohl[a][:], 0.0), [], [Ohlb[a]])
                rec = sb("rec", [64, 512], F32, ph4)
                recb = Buf()
                on = [sb("on", [64, 512], F32, ph4) for _ in range(2)]
                onb = bufs(2)
                od = sb("od", [64, 512], F32, ph4)
                odb = Buf()
                osq = sb("osq", [64, 512], BF16, ph4)
                osqb = Buf()
                olr = sb("olr", [64, 512], F32, ph4)
                olrb = Buf()
                ybf = [sb("ybf", [64, 512], BF16, ph4) for _ in range(2)]
                ybfb = bufs(2)
                yTa = sb("yTa", [128, 2, T], BF16, ph4)
                yTab = Buf()
                lmt = sb("lmt", [128, 2, 32], F32, ph4)
                lmtb = Buf()
                lm2 = sb("lm2", [128, 4], F32, ph4)
                lm2b = Buf()
                l4 = bc[:, BC_LAM:BC_LAM + 128].rearrange("p (a b e) -> p a b e", a=2, b=2)
                k.tt(lmt[:], l4[:, :, 0, :], l4[:, :, 1, :], ALU.mult, [bcb], [lmtb])
                k.op("dve", lambda g: g.tensor_reduce(lm2[:, 0:2], lmt[:], AX.X, ALU.add), [lmtb], [lm2b])
                k.act(lm2[:, 0:2], lm2[:, 0:2], AF.Exp, [lm2b], [lm2b])
                k.tt(lm2[:, 2:3], lm2[:, 1:2], lm2[:, 0:1], ALU.subtract, [lm2b], [lm2b])
                k.ts(lm2[:, 2:3], lm2[:, 2:3], -lam_init, ALU.add, [lm2b], [lm2b])
                k.ts(lm2[:, 3:4], pp[:, PP_DSUB:PP_DSUB + 1], 1.0 - lam_init, ALU.mult, [ppb], [lm2b])
                qz = [sb("qz", [128, 512], BF16, ph4) for _ in range(2)]
                qzb = bufs(2)
                def dstream(h, cp):
                    return (h // 2, 12 + (h % 2) * 2 + cp, qTd, qTdb, h // 2, h // 2, h % 2, 32 ** -0.5)

                def gstream(hq):
                    kv = hq // 2
                    return (2, 16 + kv, qTg, qTgb, hq % 2, 2, kv, 64 ** -0.5)

                groups = [("d", [dstream(h, 0), dstream(h, 1)], [(h // 2, h % 2)]) for h in range(4)]
                groups += [("g", [gstream(0), gstream(2)], [(0, 0), (1, 0)]), ("g", [gstream(1), gstream(3)], [(0, 1), (1, 1)])]
                order = [g_ for g_ in groups if g_[0] == "d"] + [g_ for g_ in groups if g_[0] == "g"]
                plan = []
                nk_ = 0
                nv_ = 0
                curK = None
                curKa = None
                for (kind, streams, outs) in order:
                    kc_ = streams[0][0]
                    kload = None
                    if curK != kc_:
                        curKa = nk_ % 2
                        nk_ += 1
                        curK = kc_
                        kload = (curKa, kc_)
                    vas, vloads = [], []
                    for st_ in streams:
                        key = (st_[5], st_[6])
                        if kind == "d" and vas:
                            vas.append(vas[0])
                            continue
                        va = nv_ % 4
                        nv_ += 1
                        vloads.append((va, key))
                        vas.append(va)
                    plan.append((kload, curKa, vas, vloads))

                def issue_loads(gi):
                    kload, _, _, vloads = plan[gi]
                    if kload is not None:
                        ka, kc_ = kload
                        k.dma("sp", Kb[ka][:].rearrange("p (r t) -> p r t", r=4), gaK[kc_].rearrange("(r p) t -> p r t", p=128),
                              [gaKb[kc_]], [Kbb[ka]], C(f"Kb{ka}"))
                    for (va, key) in vloads:
                        k.dma("sp", Vb[va][:, :, 0:64], gaV[key[0]].rearrange("(kt p) e -> p kt e", p=128)[:, :, key[1] * 64:(key[1] + 1) * 64],
                              [gaVb[key[0]]], [Vbb[va]], C(f"Vb{va}"))

                issue_loads(0)
                gi = -1
                for phase_kind in ("d", "g"):
                    for (kind, streams, outs) in groups:
                        if kind != phase_kind:
                            continue
                        gi += 1
                        if gi + 1 < len(plan):
                            issue_loads(gi + 1)
                        curKa, vas = plan[gi][1], plan[gi][2]
                        sbanks = ((0, 1, 6), (2, 3, 7))
                        for qb in range(4):
                            qsl = slice(qb * 512, (qb + 1) * 512)
                            for si, st_ in enumerate(streams):
                                (kc2, mcol, qT_, qTb_, qc, vc, vh, scl) = st_
                                k.ts(qz[si][:], qT_[:, qc, qsl], sel[:, mcol:mcol + 1], ALU.mult, [qTb_, selb], [qzb[si]])
                            seq = [(kt, si) for kt in range(64) for si in range(2)]
                            LA = 4

                            def emitS(j):
                                kt, si = seq[j]
                                scl = streams[si][7]
                                sbank = sbanks[si][kt % 3]
                                pi = si * 3 + kt % 3
                                k.mm(ps[sbank][:, :], Kb[curKa][:, kt * 128:(kt + 1) * 128], qz[si][:],
                                     [Kbb[curKa], qzb[si]], [psb[sbank]])
                                k.act(pt[pi][:], ps[sbank][:, :], AF.Exp, [psb[sbank]], [ptb[pi]], scale=scl)

                            def emitPV(j):
                                kt, si = seq[j]
                                pi = si * 3 + kt % 3
                                k.mm(ps[4 + si][0:VW, :], Vb[vas[si]][:, kt, :], pt[pi][:], [Vbb[vas[si]], ptb[pi]], [psb[4 + si]],
                                     start=(kt == 0), stop=(kt == 63))

                            for j in range(LA):
                                emitS(j)
                            for j in range(len(seq)):
                                if j + LA < len(seq):
                                    emitS(j + LA)
                                emitPV(j)
                            for cp in range(2):
                                pO_ = ps[4 + cp]
                                k.copy(Ohl[cp][64:66, 0, :], pO_[64:66, :], [psb[4 + cp]], [Ohlb[cp]])
                                k.tt(Ohl[cp][64:66, 1, :], pO_[64:66, :], Ohl[cp][64:66, 0, :], ALU.subtract, [psb[4 + cp], Ohlb[cp]], [Ohlb[cp]])
                                for hl in range(2):
                                    k.mm(ps[0][0:64, :], cb[0:VW, CB_SEL, 0:64], Ohl[cp][:, hl, :], [cbb, Ohlb[cp]], [psb[0]],
                                         start=(hl == 0), stop=(hl == 1))
                                k.op("dve", lambda g: g.reciprocal(rec[:], ps[0][0:64, :]), [psb[0]], [recb])
                                k.tt(on[cp][:], pO_[0:64, :], rec[:], ALU.mult, [psb[4 + cp], recb], [onb[cp]])
                            ys = []
                            if kind == "d":
                                k.stt(od[:], on[1][:], lm2[0:64, 2:3], on[0][:], ALU.mult, ALU.add, [onb[0], onb[1], lm2b], [odb])
                                k.act(osq[:], od[:], AF.Square, [odb], [osqb])
                                k.mm(ps[1][0:64, :], cb[0:64, CB_ONES, 0:64], osq[:], [cbb, osqb], [psb[1]])
                                k.act(olr[:], ps[1][0:64, :], AF.Ln, [psb[1]], [olrb], scale=1.0 / 64, bias=EPS)
                                k.act(olr[:], olr[:], AF.Exp, [olrb], [olrb], scale=-0.5)
                                k.stt(ybf[0][:], od[:], lm2[0:64, 3:4], olr[:], ALU.mult, ALU.mult, [odb, lm2b, olrb], [ybfb[0]])
                                ys = [0]
                            else:
                                for cp in range(2):
                                    k.copy(ybf[cp][:], on[cp][:], [onb[cp]], [ybfb[cp]])
                                ys = [0, 1]
                            for yi, (oc, oh) in zip(ys, outs):
                                if oh == 0:
                                    k.copy(yTa[0:64, oc, qsl], ybf[yi][:], [ybfb[yi]], [yTab], e="act")
                                else:
                                    k.mm(ps[1][64:128, :], cb[0:64, CB_ID, 0:64], ybf[yi][:], [cbb, ybfb[yi]], [psb[1]])
                                    k.copy(yTa[64:128, oc, qsl], ps[1][64:128, :], [psb[1]], [yTab], e="act")
                    if stop == "attn" + phase_kind:
                        dump_bf(ph4, yTa[:, 0, :], [yTab], T)
                        dump_bf(ph4, yTa[:, 1, :], [yTab], T)
                        finish()
                        return nc
                    out_proj(ph4, yTa, yTab, "w_out", l, 512 if phase_kind == "d" else 768, 2, "t" + phase_kind, pre=wo_pre[phase_kind])
              k.barrier()
            if stop == "mix":
                for i in (0, 7, 15):
                    dump(x[:, i, :], [xb[i]], D)
                finish()
                return nc
            with ExitStack() as phC:
                hT = sb("hTc", [128, KC, T], BF16, phC)
                hTb = bufs(NT, "hTc")
                wq4 = [sb("wq", [128, KC, 256], BF16, phC) for _ in range(4)]
                wq4b = bufs(4)
                wk4 = [sb("wk", [128, KC, 256], BF16, phC) for _ in range(2)]
                wk4b = bufs(2)
                for h in range(2):
                    k.dma("pool", wk4[h][:], wv_("w_ckv", l, h * 256, (h + 1) * 256), [], [wk4b[h]], C(f"wk{h}"))
                for h in range(4):
                    k.dma("pool", wq4[h][:], wv_("w_cq", l, h * 256, (h + 1) * 256), [], [wq4b[h]], C(f"wq{h}"))
                wco_pre = [load_wo(phC, "w_co", l, h * 256, 2, f"c{h}") for h in range(4)]
                with ExitStack() as ph:
                    norm_T(ph, hT, hTb, PP_NCROSS, "c")
                memT = sb("memT", [128, KC, 256], BF16, phC)
                memTb = Buf()
                KT = sb("KTm", [128, 8, 256], BF16, phC)
                KTb = Buf()
                Vm = sb("Vm", [128, 2, D], BF16, phC)
                Vmb = Buf()
                with ExitStack() as ph:
                    mt_ = sb("mt", [128, 2, D], F32, ph)
                    mtb = Buf()
                    k.dma("sp", mt_[:], W["mem"]().rearrange("(i p) d -> p i d", p=128), [], [mtb], C("mt"))
                    ss2 = sb("ss2", [128, 2], F32, ph)
                    ss2b = Buf()
                    mj = sb("mj", [128, D], BF16, ph)
                    mjb = Buf()
                    for i in range(2):
                        k.act(mj[:], mt_[:, i, :], AF.Square, [mtb], [mjb, ss2b], accum_out=ss2[:, i:i + 1])
                    k.act(ss2[:], ss2[:], AF.Ln, [ss2b], [ss2b], scale=1.0 / D, bias=EPS)
                    k.act(ss2[:], ss2[:], AF.Exp, [ss2b], [ss2b], scale=-0.5)
                    for i in range(2):
                        k.ts(mj[:], mt_[:, i, :], ss2[:, i:i + 1], ALU.mult, [mtb, ss2b], [mjb])
                        pv = psbf(6 + i)
                        for kc in range(KC):
                            k.tr(pv[:, kc * 128:(kc + 1) * 128], mj[:, kc * 128:(kc + 1) * 128], ident, [mjb, cbb], [psb[6 + i]])
                        k.tt(memT[:, :, i * 128:(i + 1) * 128], pv.rearrange("p (k t) -> p k t", k=KC),
                             bcast(pp[:, PP_NMEM:PP_NMEM + KC].unsqueeze(2), [128, KC, 128]), ALU.mult, [psb[6 + i], ppb], [memTb])
                    wk, wkb = wk4, wk4b
                    sqk = sb("sqk", [128, 512], BF16, ph)
                    sqkb = Buf()
                    rk = sb("rk", [128, 256], F32, ph)
                    rkb = Buf()
                    for h in range(4):
                        a = h % 2
                        if h >= 2:
                            k.dma("pool", wk[a][:], wv_("w_ckv", l, h * 256, (h + 1) * 256), [], [wkb[a]], C(f"wk{a}"))
                        for c in range(2):
                            for kc in range(KC):
                                k.mm(ps[a][:, c * 256:(c + 1) * 256], wk[a][:, kc, c * 128:(c + 1) * 128], memT[:, kc, :],
                                     [wkb[a], memTb], [psb[a]], start=(kc == 0), stop=(kc == KC - 1))
                        k.act(sqk[:], ps[a][:, :], AF.Square, [psb[a]], [sqkb])
                        for c in range(2):
                            k.mm(ps[2][:, 0:256], cb[:, CB_ONES, :], sqk[:, c * 256:(c + 1) * 256], [cbb, sqkb], [psb[2]],
                                 start=(c == 0), stop=(c == 1))
                        k.act(rk[:], ps[2][:, 0:256], AF.Ln, [psb[2]], [rkb], scale=1.0 / 256, bias=EPS)
                        k.act(rk[:], rk[:], AF.Exp, [rkb], [rkb], scale=-0.5)
                        for c in range(2):
                            k.stt(KT[:, 2 * h + c, :], ps[a][:, c * 256:(c + 1) * 256], pp[:, PP_CKN + c:PP_CKN + c + 1], rk[:],
                                  ALU.mult, ALU.mult, [psb[a], ppb, rkb], [KTb])
                    wvv = sb("wvv", [128, KC, D], BF16, ph)
                    wvvb = Buf()
                    k.dma("pool", wvv[:], wv_("w_ckv", l, D, 2 * D), [], [wvvb], C("wvv"))
                    for i in range(2):
                        for half in range(2):
                            pb = 3 + half
                            for kc in range(KC):
                                k.mm(ps[pb][:, :], memT[:, kc, i * 128:(i + 1) * 128], wvv[:, kc, half * 512:(half + 1) * 512],
                                     [memTb, wvvb], [psb[pb]], start=(kc == 0), stop=(kc == KC - 1))
                            k.copy(Vm[:, i, half * 512:(half + 1) * 512], ps[pb][:, :], [psb[pb]], [Vmb], e="act")
                    k.barrier()
                with ExitStack() as ph:
                    wq, wqb = wq4, wq4b
                    sqq = sb("sqq", [128, 2, 512], BF16, ph)
                    sqqb = Buf()
                    rq = sb("rq", [128, 512], F32, ph)
                    rqb = Buf()
                    qn = sb("qn", [128, 2, 512], BF16, ph)
                    qnb = Buf()
                    ptc = [sb("ptc", [128, 512], BF16, ph) for _ in range(2)]
                    ptcb = bufs(2)
                    rcc = sb("rcc", [128, 512], F32, ph)
                    rccb = Buf()
                    coT = sb("coT", [128, 2, T], BF16, ph)
                    coTb = Buf()
                    for h in range(4):
                        a = h
                        def emit_qproj(tb):
                            tsl = slice(tb * 512, (tb + 1) * 512)
                            for c in range(2):
                                for kc in range(KC):
                                    k.mm(ps[c][:, :], wq[a][:, kc, c * 128:(c + 1) * 128], hT[:, kc, tsl],
                                         [wqb[a]] + hTb[4 * tb:4 * tb + 4], [psb[c]], start=(kc == 0), stop=(kc == KC - 1))
                                k.act(sqq[:, c, :], ps[c][:, :], AF.Square, [psb[c]], [sqqb])

                        def emit_qnorm():
                            for c in range(2):
                                k.mm(ps[2][:, :], cb[:, CB_ONES, :], sqq[:, c, :], [cbb, sqqb], [psb[2]], start=(c == 0), stop=(c == 1))
                            k.act(rq[:], ps[2][:, :], AF.Ln, [psb[2]], [rqb], scale=1.0 / 256, bias=EPS)
                            k.act(rq[:], rq[:], AF.Exp, [rqb], [rqb], scale=-0.5)
                            for c in range(2):
                                k.stt(qn[:, c, :], ps[c][:, :], pp[:, PP_CQN + c:PP_CQN + c + 1], rq[:], ALU.mult, ALU.mult,
                                      [psb[c], ppb, rqb], [qnb])

                        emit_qproj(0)
                        emit_qnorm()
                        for tb in range(4):
                            tsl = slice(tb * 512, (tb + 1) * 512)
                            for mi in range(2):
                                for c in range(2):
                                    k.mm(ps[3 + mi][:, :], KT[:, 2 * h + c, mi * 128:(mi + 1) * 128], qn[:, c, :], [KTb, qnb], [psb[3 + mi]],
                                         start=(c == 0), stop=(c == 1))
                                k.act(ptc[mi][:], ps[3 + mi][:, :], AF.Exp, [psb[3 + mi]], [ptcb[mi]], scale=1.0 / 16)
                            if tb + 1 < 4:
                                emit_qproj(tb + 1)
                                emit_qnorm()
                            for mi in range(2):
                                for dv in range(2):
                                    k.mm(ps[5 + dv][:, :], Vm[:, mi, h * 256 + dv * 128:h * 256 + (dv + 1) * 128], ptc[mi][:],
                                         [Vmb, ptcb[mi]], [psb[5 + dv]], start=(mi == 0), stop=(mi == 1))
                                k.mm(ps[7][:, :], cb[:, CB_ONES, :], ptc[mi][:], [cbb, ptcb[mi]], [psb[7]], start=(mi == 0), stop=(mi == 1))
                            k.op("dve", lambda g: g.reciprocal(rcc[:], ps[7][:, :]), [psb[7]], [rccb])
                            for dv in range(2):
                                k.tt(coT[:, dv, tsl], ps[5 + dv][:, :], rcc[:], ALU.mult, [psb[5 + dv], rccb], [coTb])
                        out_proj(ph, coT, coTb, "w_co", l, h * 256, 2, f"c{h}", pre=wco_pre[h])
                    k.barrier()
            if stop == "cross":
                for i in (0, 7, 15):
                    dump(x[:, i, :], [xb[i]], D)
                finish()
                return nc
            with ExitStack() as phF:
                hT = sb("hTf", [128, KC, T], BF16, phF)
                hTb = bufs(NT, "hTf")
                wfi2 = [sb("wfi", [128, KC, 2, 512], BF16, phF) for _ in range(2)]
                wfib2 = bufs(2)
                wfo2 = [sb("wfo", [128, 4, D], BF16, phF) for _ in range(2)]
                wfob2 = bufs(2)

                def load_ffn_group(grp):
                    ga_ = grp % 2
                    nch_ = 4 if grp < 5 else 2
                    k.dma("pool", wfi2[ga_][:, :, 0, 0:nch_ * 128], wv_("w_fi", l, grp * 512, grp * 512 + nch_ * 128), [], [wfib2[ga_]], C(f"wfi{ga_}"))
                    k.dma("pool", wfi2[ga_][:, :, 1, 0:nch_ * 128], wv_("w_fi", l, DFF + grp * 512, DFF + grp * 512 + nch_ * 128), [], [wfib2[ga_]], C(f"wfi{ga_}"))
                    k.dma("pool", wfo2[ga_][:, 0:nch_, :], W["w_fo"]()[l][grp * 512:grp * 512 + nch_ * 128, :].rearrange("(k p) c -> p k c", p=128),
                          [], [wfob2[ga_]], C(f"wfo{ga_}"))

                with ExitStack() as ph:
                    norm_T(ph, hT, hTb, PP_NFFN, "f")
                hx = sb("hx", [128, KC, 2], BF16, phF)
                hxb = Buf()
                with ExitStack() as ph:
                    he = sb("he", [128, 16], F32, ph)
                    heb = Buf()
                    k.copy(he[:, 0:8], hT[:, :, 0], [hTb[0]], [heb])
                    k.copy(he[:, 8:16], hT[:, :, T - 1], [hTb[NT - 1]], [heb])
                    k.dma("sp", exH, he[:], [heb], [exHb], C("exH"))
                    allgather(exH, gaH, exHb, gaHb, "agH")
                    load_ffn_group(0)
                    H4 = sb("H4", [128, 4, 16], F32, ph)
                    H4b = Buf()
                    k.dma("sp", H4[:], gaH.rearrange("(r p) c -> p r c", p=128), [gaHb], [H4b], C("H4"))
                    hs_ = sb("hs", [128, 2, 8], F32, ph)
                    hsb = Buf()
                    for side, (s0, c0) in enumerate(((4, 8), (8, 0))):
                        k.ts(hs_[:, side, :], H4[:, 0, c0:c0 + 8], sel[:, s0:s0 + 1], ALU.mult, [H4b, selb], [hsb])
                        for r in range(1, 4):
                            k.stt(hs_[:, side, :], H4[:, r, c0:c0 + 8], sel[:, s0 + r:s0 + r + 1], hs_[:, side, :], ALU.mult, ALU.add,
                                  [H4b, selb, hsb], [hsb])
                    k.copy(hx[:].rearrange("p k s -> p s k"), hs_[:], [hsb], [hxb])
                    k.barrier()
                with ExitStack() as ph:
                    aT = sb("aT", [128, 4, T], BF16, ph)
                    aTb = Buf()
                    gext = sb("gext", [128, T + 2], F32, ph)
                    gextb = Buf()
                    gc = sb("gc", [128, T], F32, ph)
                    gcb = Buf()
                    sg = sb("sg", [128, T], F32, ph)
                    sgb = Buf()
                    npb = 0
                    for grp in range(6):
                        nch = 4 if grp < 5 else 2
                        if grp + 1 < 6:
                            load_ffn_group(grp + 1)
                        wfi, wfib, wfo, wfob = wfi2[grp % 2], wfib2[grp % 2], wfo2[grp % 2], wfob2[grp % 2]
                        for fi in range(nch):
                            f = grp * 4 + fi
                            fsl = slice(fi * 128, (fi + 1) * 128)
                            for tb in range(4):
                                tsl = slice(tb * 512, (tb + 1) * 512)
                                pb = npb % 4
                                npb += 1
                                for kc in range(KC):
                                    k.mm(ps[pb][:, :], wfi[:, kc, 0, fsl], hT[:, kc, tsl], [wfib] + hTb[4 * tb:4 * tb + 4], [psb[pb]],
                                         start=(kc == 0), stop=(kc == KC - 1))
                                k.copy(gext[:, 1 + tb * 512:1 + (tb + 1) * 512], ps[pb][:, :], [psb[pb]], [gextb], e="act")
                            for kc in range(KC):
                                k.mm(ps[4][:, 0:2], wfi[:, kc, 0, fsl], hx[:, kc, :], [wfib, hxb], [psb[4]], start=(kc == 0), stop=(kc == KC - 1))
                            k.copy(gext[:, 0:1], ps[4][:, 0:1], [psb[4]], [gextb], e="act")
                            k.copy(gext[:, T + 1:T + 2], ps[4][:, 1:2], [psb[4]], [gextb], e="act")
                            cw = PP_CONV + 4 * f
                            k.ts(gc[:], gext[:, 1:T + 1], pp[:, cw + 1:cw + 2], ALU.mult, [gextb, ppb], [gcb], s2=pp[:, cw + 3:cw + 4], op1=ALU.add)
                            k.stt(gc[:], gext[:, 0:T], pp[:, cw:cw + 1], gc[:], ALU.mult, ALU.add, [gextb, ppb, gcb], [gcb])
                            k.stt(gc[:], gext[:, 2:T + 2], pp[:, cw + 2:cw + 3], gc[:], ALU.mult, ALU.add, [gextb, ppb, gcb], [gcb])
                            k.act(sg[:], gc[:], AF.Silu, [gcb], [sgb])
                            for tb in range(4):
                                tsl = slice(tb * 512, (tb + 1) * 512)
                                pb = npb % 4
                                npb += 1
                                for kc in range(KC):
                                    k.mm(ps[pb][:, :], wfi[:, kc, 1, fsl], hT[:, kc, tsl], [wfib] + hTb[4 * tb:4 * tb + 4], [psb[pb]],
                                         start=(kc == 0), stop=(kc == KC - 1))
                                k.tt(aT[:, fi, tsl], sg[:, tsl], ps[pb][:, :], ALU.mult, [sgb, psb[pb]], [aTb])
                        n = 0
                        for i in range(NT):
                            for half in range(2):
                                pb = 6 + (n % 2)
                                n += 1
                                for fi in range(nch):
                                    k.mm(ps[pb][:, :], aT[:, fi, i * 128:(i + 1) * 128], wfo[:, fi, half * 512:(half + 1) * 512],
                                         [aTb, wfob], [psb[pb]], start=(fi == 0), stop=(fi == nch - 1))
                                resid_add(i, half, pb)
                    k.barrier()
            if stop == "ffn" and l == 0:
                for i in (0, 7, 15):
                    dump(x[:, i, :], [xb[i]], D)
                finish()
                return nc
        yb_ = Buf()
        yv_ = y_d.rearrange("(i p) d -> p i d", p=128)
        for q in range(4):
            k.dma("sp", yv_[:, 4 * q:4 * q + 4, :], x[:, 4 * q:4 * q + 4, :], xb[4 * q:4 * q + 4], [yb_], C(f"y{q}"))
        finish()
    return nc


def _host_consts():
    cm = np.zeros((CM_N, 128, 128), np.float32)
    s = np.arange(128)[:, None]
    t = np.arange(128)[None, :]
    same = (s // 64) == (t // 64)
    cm[CM_MASKF] = (same & (s <= t))
    cm[CM_MASKB] = (same & (s >= t))
    cm[CM_CH0LO] = ((s < 64) & (t < 64))
    cm[CM_CH0HI] = ((s < 64) & (t >= 64))
    cm[CM_CH1LO] = ((s >= 64) & (t < 64))
    cm[CM_CH1HI] = ((s >= 64) & (t >= 64))
    cm[CM_CGE] = ((s // 64) >= (t // 64))
    cm[CM_CLE] = ((s // 64) <= (t // 64))
    cm[CM_ONES] = 1.0
    cb = np.zeros((CB_N, 128, 128), np.float32)
    cb[CB_ID] = np.eye(128)
    for m in range(128):
        if (m % 32) < 16:
            cb[CB_RPT, m + 16, m] = -1.0
        else:
            cb[CB_RPT, m - 16, m] = 1.0
    cb[CB_B32] = ((s // 32) == (t // 32))
    cb[CB_B64] = same
    cb[CB_ONES] = 1.0
    cb[CB_MASKF] = cm[CM_MASKF]
    cb[CB_MASKB] = cm[CM_MASKB]
    cb[CB_SEL, 64, 0:64] = 1.0
    return cm, cb


def _bands(j):
    bands = np.zeros((20, 128, 128), np.float32)
    s = np.arange(128)[:, None]
    t = np.arange(128)[None, :]
    for g, w in enumerate((2, 4, 8, 16)):
        half = w // 2
        lo = t - half
        hi = t + half - 1
        for which, off in enumerate((-128, 0, 128)):
            sg = s + off
            m = ((sg >= lo) & (sg <= hi)).astype(np.float32) / w
            if which == 1:
                m = m - (s == t)
            bands[3 * g + which] = m
        first = bands[3 * g + 1].copy()
        last = bands[3 * g + 1].copy()
        if j == 0:
            lo_c = np.maximum(lo, 0)
            cnt = (hi - lo_c + 1).astype(np.float32)
            first = ((s >= lo_c) & (s <= hi)).astype(np.float32) / cnt - (s == t)
        if j == 3:
            hi_c = np.minimum(hi, 127)
            cnt = (hi_c - lo + 1).astype(np.float32)
            last = ((s >= lo) & (s <= hi_c)).astype(np.float32) / cnt - (s == t)
        bands[12 + g] = first
        bands[16 + g] = last
    return bands


def _prep_inputs(inp):
    f = lambda a: np.ascontiguousarray(np.asarray(a, dtype=np.float32))
    cm, cb = _host_consts()
    pp = np.zeros((L, 128, PP_N), np.float32)
    bcv = np.zeros((L, 128, BC_N), np.float32)
    p = np.arange(128)
    for l in range(L):
        for base, key in ((PP_NMIX, "norm_mix"), (PP_NCROSS, "norm_cross"), (PP_NFFN, "norm_ffn"), (PP_NMEM, "norm_mem")):
            pp[l, :, base:base + 8] = f(inp[key])[l].reshape(8, 128).T
        pp[l, :, PP_DQ] = f(inp["diff_qnorm"])[l][p % 32]
        pp[l, :, PP_DK] = f(inp["diff_knorm"])[l][p % 32]
        pp[l, :, PP_GQ] = f(inp["gqa_qnorm"])[l][p % 64]
        pp[l, :, PP_GK] = f(inp["gqa_knorm"])[l][p % 64]
        pp[l, :, PP_DSUB] = f(inp["diff_subnorm"])[l][p % 64]
        pp[l, :, PP_PSC:PP_PSC + 2] = f(inp["pool_scale"])[l].reshape(2, 128).T
        pp[l, :, PP_CQN:PP_CQN + 2] = f(inp["cross_qnorm"])[l].reshape(2, 128).T
        pp[l, :, PP_CKN:PP_CKN + 2] = f(inp["cross_knorm"])[l].reshape(2, 128).T
        cw = f(inp["ffn_conv"])[l]
        cbias = f(inp["ffn_conv_b"])[l]
        for c in range(NFC):
            for j in range(3):
                pp[l, :, PP_CONV + 4 * c + j] = cw[j, c * 128:(c + 1) * 128]
            pp[l, :, PP_CONV + 4 * c + 3] = cbias[c * 128:(c + 1) * 128]
        bi = f(inp["ml_bias_i"])[l]
        bf_ = f(inp["ml_bias_f"])[l]
        gb = np.concatenate([bi[0], bf_[0], bi[1], bf_[1]])
        bcv[l, :, BC_GB:BC_GB + 16] = gb[None, :]
        bcv[l, :, BC_MLN:BC_MLN + 256] = f(inp["ml_norm"])[l][None, :]
        bcv[l, :, BC_LAM:BC_LAM + 128] = f(inp["diff_lambda"])[l].reshape(-1)[None, :]
    inv32 = (10000.0 ** (-np.arange(0, 32, 2, dtype=np.float32) / 32)).astype(np.float32)
    common = dict(w_in=f(inp["w_in"]), w_out=f(inp["w_out"]), w_cq=f(inp["w_cq"]), w_ckv=f(inp["w_ckv"]),
                  w_co=f(inp["w_co"]), w_ffn_in=f(inp["w_ffn_in"]), w_ffn_out=f(inp["w_ffn_out"]),
                  pool_w=f(inp["pool_w"]), pp=pp, bc=bcv, cmatb=cb)
    xs = f(inp["x"])
    mems = f(inp["mem"])
    maps = []
    for c in range(NCORES):
        b, j = c // 4, c % 4
        pos = np.arange(j * T, (j + 1) * T)
        d = np.arange(128)
        angD = pos[None, :].astype(np.float32) * inv32[(d % 32) % 16][:, None]
        rowp = (pos // 64).astype(np.float32)
        colp = (pos % 64).astype(np.float32)
        isrow = ((d % 64) < 32)[:, None]
        angG = np.where(isrow, rowp[None, :], colp[None, :]).astype(np.float32) * inv32[(d % 32) % 16][:, None]
        rope = np.stack([np.cos(angD), np.sin(angD), np.cos(angG), np.sin(angG)]).astype(np.float32)
        selv = np.zeros((128, 18), np.float32)
        for jj in range(4):
            selv[32 * jj:32 * jj + 32, 12 + jj] = 1.0
        for jj in range(2):
            selv[64 * jj:64 * jj + 64, 16 + jj] = 1.0
        selv[:, 0 + j] = 1.0
        if j > 0:
            selv[:, 4 + (j - 1)] = 1.0
        if j < 3:
            selv[:, 8 + (j + 1)] = 1.0
        cmc = np.concatenate([cm[0:9], _bands(j)], 0)
        m = dict(common)
        m.update(x=np.ascontiguousarray(xs[b, j * T:(j + 1) * T]), mem=np.ascontiguousarray(mems[b]),
                 sel=selv, cmat=cmc, rope=rope)
        maps.append(m)
    return maps


_NC_CACHE = {}


def kernel(**inputs):
    maps = _prep_inputs(inputs)
    if "nc" not in _NC_CACHE:
        _NC_CACHE["nc"] = build()
    nc = _NC_CACHE["nc"]
    maps = [{kk: vv for kk, vv in m.items() if kk in nc._declared} for m in maps]
    res = run_bass_kernel_spmd(nc, maps, core_ids=list(range(NCORES)))
    out = np.zeros((2, 4 * T, D), np.float32)
    for c in range(NCORES):
        out[c // 4, (c % 4) * T:(c % 4 + 1) * T] = res.results[c]["y"]
    return out
```
